# Optimizing a Trainium2 kernel written in Bass

```python
import math
import jax, jax.numpy as jnp
from jax import lax
import numpy as np

D_MODEL = 4096
BATCH = 4
SEQ = 4096
DEPTH = 1

MIX_WIDTH = D_MODEL
ATTN_WIDTH = MIX_WIDTH // 2
POOL_WIDTH = MIX_WIDTH - ATTN_WIDTH
DIFF_HEAD_DIM = 128
N_DIFF_HEADS = ATTN_WIDTH // (2 * DIFF_HEAD_DIM)
V_HEAD_DIM = 2 * DIFF_HEAD_DIM
ROPE_DIM = DIFF_HEAD_DIM // 4
ROPE_THETA = 500000.0
Q_BLOCK = 128
SUBLN_EPS = 1e-5
POOL_WINDOWS = (2, 4, 8, 16)
N_POOL_GROUPS = len(POOL_WINDOWS)
POOL_GROUP_DIM = POOL_WIDTH // N_POOL_GROUPS
MAX_WINDOW = max(POOL_WINDOWS)
IN_WIDTH = 3 * ATTN_WIDTH + POOL_WIDTH
D_FF = ((8 * D_MODEL // 3 + 255) // 256) * 256
CONV_WIDTH = 3
NORM_EPS = 1e-5
POS_OFFSET_MAX = 1024

kernel_name = 'hymba_diffattn_pool_convffn_block'


def rms_norm(x, g, eps):
    x32 = x.astype(jnp.float32)
    y = x32 * lax.rsqrt(jnp.mean(x32 * x32, axis=-1, keepdims=True) + eps)
    return (y * g.astype(jnp.float32)).astype(x.dtype)


def apply_partial_rope(t, cos, sin):
    half = ROPE_DIM // 2
    t32 = t[..., :ROPE_DIM].astype(jnp.float32)
    t1, t2 = t32[..., :half], t32[..., half:]
    rot = jnp.concatenate([t1 * cos - t2 * sin, t2 * cos + t1 * sin], axis=-1).astype(t.dtype)
    return jnp.concatenate([rot, t[..., ROPE_DIM:]], axis=-1)


def diff_attention(q, k, v, lam, subln_g, lambda_init):
    B, S = q.shape[0], q.shape[1]
    q = q * (DIFF_HEAD_DIM ** -0.5)
    outs = []
    for blk in range(S // Q_BLOCK):
        q0 = blk * Q_BLOCK
        kv_len = q0 + Q_BLOCK
        qb = q[:, q0:kv_len]
        kb = k[:, :kv_len]
        vb = v[:, :kv_len]
        s = jnp.einsum('bqhcd,bkhcd->bhcqk', qb, kb).astype(jnp.float32)
        q_idx = q0 + jnp.arange(Q_BLOCK)
        k_idx = jnp.arange(kv_len)
        mask = k_idx[None, :] <= q_idx[:, None]
        s = jnp.where(mask, s, -1e30)
        p = jax.nn.softmax(s, axis=-1)
        a = (p[:, :, 0] - lam * p[:, :, 1]).astype(v.dtype)
        outs.append(jnp.einsum('bhqk,bkhe->bqhe', a, vb))
    o = jnp.concatenate(outs, axis=1)
    o = rms_norm(o, subln_g, SUBLN_EPS) * (1.0 - lambda_init)
    return o.reshape(B, S, ATTN_WIDTH)


def multiscale_pool(u, w_pool, pool_scale):
    B, S, _ = u.shape
    u32 = u.astype(jnp.float32).reshape(B, S, N_POOL_GROUPS, POOL_GROUP_DIM)
    c = jnp.cumsum(u32, axis=1)
    c_pad = jnp.pad(c, ((0, 0), (MAX_WINDOW, 0), (0, 0), (0, 0)))
    counts_base = jnp.arange(1, S + 1)
    groups = []
    for g, w in enumerate(POOL_WINDOWS):
        lagged = c_pad[:, MAX_WINDOW - w:MAX_WINDOW - w + S, g]
        count = jnp.minimum(counts_base, w).astype(jnp.float32)
        mean = (c[:, :, g] - lagged) / count[None, :, None]
        groups.append(mean - u32[:, :, g])
    pooled = jnp.stack(groups, axis=2).astype(u.dtype)
    y = jnp.einsum('bsgc,gcd->bsgd', pooled, w_pool).reshape(B, S, POOL_WIDTH)
    return y * pool_scale


def setup_inputs(seed: int = 0) -> dict:
    key = jax.random.key(seed)
    ks = jax.random.split(key, 20)
    f32 = jnp.float32
    x = jax.random.normal(ks[0], (BATCH, SEQ, D_MODEL), f32)
    offsets = jax.random.randint(ks[1], (BATCH, 1), 0, POS_OFFSET_MAX, dtype=jnp.int32)
    positions = (offsets + jnp.arange(SEQ, dtype=jnp.int32)[None, :]).astype(jnp.int32)
    norm1_g = 1.0 + 0.02 * jax.random.normal(ks[2], (DEPTH, D_MODEL), f32)
    w_in = jax.random.normal(ks[3], (DEPTH, D_MODEL, IN_WIDTH), f32) * D_MODEL ** -0.5
    lambda_q1 = 0.1 * jax.random.normal(ks[4], (DEPTH, DIFF_HEAD_DIM), f32)
    lambda_k1 = 0.1 * jax.random.normal(ks[5], (DEPTH, DIFF_HEAD_DIM), f32)
    lambda_q2 = 0.1 * jax.random.normal(ks[6], (DEPTH, DIFF_HEAD_DIM), f32)
    lambda_k2 = 0.1 * jax.random.normal(ks[7], (DEPTH, DIFF_HEAD_DIM), f32)
    subln_g = 1.0 + 0.02 * jax.random.normal(ks[8], (DEPTH, V_HEAD_DIM), f32)
    w_pool = jax.random.normal(ks[9], (DEPTH, N_POOL_GROUPS, POOL_GROUP_DIM, POOL_GROUP_DIM), f32) * POOL_GROUP_DIM ** -0.5
    pool_scale = 1.0 + 0.02 * jax.random.normal(ks[10], (DEPTH, POOL_WIDTH), f32)
    w_out = jax.random.normal(ks[11], (DEPTH, MIX_WIDTH, D_MODEL), f32) * MIX_WIDTH ** -0.5
    norm2_g = 1.0 + 0.02 * jax.random.normal(ks[12], (DEPTH, D_MODEL), f32)
    w_gate = jax.random.normal(ks[13], (DEPTH, D_MODEL, D_FF), f32) * D_MODEL ** -0.5
    w_up = jax.random.normal(ks[14], (DEPTH, D_MODEL, D_FF), f32) * D_MODEL ** -0.5
    conv_w = jax.random.normal(ks[15], (DEPTH, CONV_WIDTH, D_FF), f32) * CONV_WIDTH ** -0.5
    conv_b = 0.01 * jax.random.normal(ks[16], (DEPTH, D_FF), f32)
    w_down = jax.random.normal(ks[17], (DEPTH, D_FF, D_MODEL), f32) * D_FF ** -0.5
    norm_f_g = 1.0 + 0.02 * jax.random.normal(ks[18], (D_MODEL,), f32)
    return {'x': x, 'positions': positions, 'norm1_g': norm1_g, 'w_in': w_in,
            'lambda_q1': lambda_q1, 'lambda_k1': lambda_k1, 'lambda_q2': lambda_q2, 'lambda_k2': lambda_k2,
            'subln_g': subln_g, 'w_pool': w_pool, 'pool_scale': pool_scale, 'w_out': w_out,
            'norm2_g': norm2_g, 'w_gate': w_gate, 'w_up': w_up, 'conv_w': conv_w, 'conv_b': conv_b,
            'w_down': w_down, 'norm_f_g': norm_f_g}


def reference(x, positions, norm1_g, w_in, lambda_q1, lambda_k1, lambda_q2, lambda_k2, subln_g,
              w_pool, pool_scale, w_out, norm2_g, w_gate, w_up, conv_w, conv_b, w_down, norm_f_g):
    B, S, _ = x.shape
    inv_freq = 1.0 / (ROPE_THETA ** (jnp.arange(0, ROPE_DIM, 2, dtype=jnp.float32) / ROPE_DIM))
    angles = positions.astype(jnp.float32)[..., None] * inv_freq
    cos = jnp.cos(angles)[:, :, None, None, :]
    sin = jnp.sin(angles)[:, :, None, None, :]
    h = x
    for l in range(DEPTH):
        lambda_init = 0.8 - 0.6 * math.exp(-0.3 * l)
        n = rms_norm(h, norm1_g[l], NORM_EPS)
        proj = n @ w_in[l]
        q, k, v, u = jnp.split(proj, [ATTN_WIDTH, 2 * ATTN_WIDTH, 3 * ATTN_WIDTH], axis=-1)
        q = apply_partial_rope(q.reshape(B, S, N_DIFF_HEADS, 2, DIFF_HEAD_DIM), cos, sin)
        k = apply_partial_rope(k.reshape(B, S, N_DIFF_HEADS, 2, DIFF_HEAD_DIM), cos, sin)
        v = v.reshape(B, S, N_DIFF_HEADS, V_HEAD_DIM)
        lam = (jnp.exp(jnp.sum(lambda_q1[l].astype(jnp.float32) * lambda_k1[l].astype(jnp.float32)))
               - jnp.exp(jnp.sum(lambda_q2[l].astype(jnp.float32) * lambda_k2[l].astype(jnp.float32)))
               + lambda_init)
        attn_out = diff_attention(q, k, v, lam, subln_g[l], lambda_init)
        pool_out = multiscale_pool(u, w_pool[l], pool_scale[l])
        h = h + jnp.concatenate([attn_out, pool_out], axis=-1) @ w_out[l]
        n2 = rms_norm(h, norm2_g[l], NORM_EPS)
        gate = n2 @ w_gate[l]
        up = n2 @ w_up[l]
        gp = jnp.pad(gate, ((0, 0), (CONV_WIDTH - 1, 0), (0, 0)))
        cw = conv_w[l]
        conv = conv_b[l] + cw[0] * gp[:, 0:S]
        for j in range(1, CONV_WIDTH):
            conv = conv + cw[j] * gp[:, j:j + S]
        h = h + (jax.nn.silu(conv) * up) @ w_down[l]
    return rms_norm(h, norm_f_g, NORM_EPS)
```

```python
import math
from contextlib import ExitStack

import numpy as np
import concourse.bass as bass
import concourse.mybir as mybir
from concourse.bass_utils import run_bass_kernel_spmd

F32 = mybir.dt.float32
BF16 = mybir.dt.bfloat16
I32 = mybir.dt.int32
AF = mybir.ActivationFunctionType
ALU = mybir.AluOpType
AX = mybir.AxisListType

P = 128


class Cfg:
    def __init__(self, D=4096, SEQ=4096, DFF=11008, BATCH=4):
        self.D, self.SEQ, self.DFF, self.BATCH = D, SEQ, DFF, BATCH
        self.DC = D // P
        self.AW = D // 2
        self.NH = self.AW // 256
        self.NHC = self.NH * 2
        self.PW = D - self.AW
        self.PG = self.PW // 4
        self.PGC = self.PG // P
        self.INW = 3 * self.AW + self.PW
        self.FC = DFF // P
        self.NB = SEQ // P
        self.HALF = SEQ // 2
        self.T1 = 512
        self.NT1 = SEQ // self.T1
        self.OWN_T0 = (self.HALF - P) // self.T1
        self.OWN_S0 = self.OWN_T0 * self.T1
        self.QB0 = self.HALF // P - 1
        self.NQB = self.NB - self.QB0
        self.H0 = self.HALF - P
        self.NOWN = self.HALF + P
        self.NMAIN = self.HALF // 512


class Buf:
    __slots__ = ("t", "w", "r", "pr", "sem", "cnt", "name")

    def __init__(self, t, name):
        self.t, self.name = t, name
        self.w, self.r, self.pr = {}, {}, {}
        self.sem, self.cnt = None, 0


class KB:
    def __init__(self, nc, es):
        self.nc, self.es = nc, es
        self.E = {"pe": nc.tensor, "act": nc.scalar, "dve": nc.vector, "pool": nc.gpsimd, "sp": nc.sync}
        self.sem = {k: es.enter_context(nc.semaphore("e_" + k)) for k in ("pe", "act", "dve", "pool")}
        self.cnt = {k: 0 for k in self.sem}
        self.waited = {k: {} for k in self.E}
        self.owners = []
        self.nbuf = 0

    def sb(self, stack, shape, dtype, name):
        self.nbuf += 1
        nm = "%s_%d" % (name, self.nbuf)
        return Buf(stack.enter_context(self.nc.sbuf_tensor(nm, list(shape), dtype)), nm)

    def ps(self, stack, shape, dtype, name):
        self.nbuf += 1
        nm = "%s_%d" % (name, self.nbuf)
        return Buf(stack.enter_context(self.nc.psum_tensor(nm, list(shape), dtype)), nm)

    def view(self, ap, name):
        self.nbuf += 1
        return Buf(ap, "%s_%d" % (name, self.nbuf))

    def _wait(self, e, deps):
        for key, (sem, val) in deps.items():
            if self.waited[e].get(key, 0) < val:
                self.E[e].wait_ge(sem, val)
                self.waited[e][key] = val

    @staticmethod
    def _deps(reads, writes, accs):
        d = {}

        def add(dic):
            for k, (sem, v) in dic.items():
                if k not in d or d[k][1] < v:
                    d[k] = (sem, v)
        for b in reads:
            add(b.w)
        for b in writes:
            add(b.w)
            add(b.r)
        for b in accs:
            add(b.r)
            add(b.pr)
        return d

    @staticmethod
    def _record(ev, reads, writes, accs):
        k, sem, v = ev
        for b in writes:
            b.w = {k: (sem, v)}
            b.pr = b.r
            b.r = {}
        for b in accs:
            if k not in b.w or b.w[k][1] < v:
                b.w[k] = (sem, v)
        for b in reads:
            if k not in b.r or b.r[k][1] < v:
                b.r[k] = (sem, v)

    def op(self, e, fn, reads=(), writes=(), accs=()):
        self._wait(e, self._deps(reads, writes, accs))
        ins = fn()
        self.cnt[e] += 1
        ins.then_inc(self.sem[e], 1)
        self._record((e, self.sem[e], self.cnt[e]), reads, writes, accs)
        return ins

    def pe_group(self, fns, reads=(), writes=(), accs=()):
        self._wait("pe", self._deps(reads, writes, accs))
        ins = None
        for fn in fns:
            ins = fn()
        self.cnt["pe"] += 1
        ins.then_inc(self.sem["pe"], 1)
        self._record(("pe", self.sem["pe"], self.cnt["pe"]), reads, writes, accs)

    def dma(self, q, out, in_, owner, reads=(), writes=(), accs=()):
        self._wait(q, self._deps(reads, writes, accs))
        if owner.sem is None:
            owner.sem = self.es.enter_context(self.nc.semaphore("d_" + owner.name))
            self.owners.append(owner)
        owner.cnt += 16
        self.E[q].dma_start(out=out, in_=in_).then_inc(owner.sem, 16)
        self._record(("d_" + owner.name, owner.sem, owner.cnt), reads, writes, accs)

    def barrier(self):
        ev = {k: (self.sem[k], self.cnt[k]) for k in self.sem if self.cnt[k] > 0}
        for b in self.owners:
            ev["d_" + b.name] = (b.sem, b.cnt)
        for e in self.E:
            self._wait(e, ev)


def build(cfg):
    c = cfg
    D, DC, SEQ, AW, NH, NHC, PGC, FC, NB = c.D, c.DC, c.SEQ, c.AW, c.NH, c.NHC, c.PGC, c.FC, c.NB
    nc = bass.Bass("TRN2", target_bir_lowering=False)

    def din(name, shape, dt=F32):
        return nc.dram_tensor(name, list(shape), dt, kind="ExternalInput").ap()

    xs = din("xs", [SEQ, D])
    posr = din("posr", [32, SEQ], I32)
    ropec = din("ropec", [32, 2])
    kvalid = din("kvalid", [P, NB])
    invcnt = din("invcnt", [P, 4, SEQ - c.OWN_S0])
    cst = din("cst", [P, 3 * P])
    gfrep = din("gfrep", [P, D])
    lamrep = din("lamrep", [P, 4, P])
    slncol = din("slncol", [P, 2])
    g1col = din("g1col", [P, DC])
    g2col = din("g2col", [P, DC])
    pscol = din("pscol", [P, 4 * PGC])
    cwcol = din("cwcol", [P, FC, 3])
    cbcol = din("cbcol", [P, FC])
    w_in = din("w_in", [D, c.INW])
    w_pool = din("w_pool", [4, c.PG, c.PG])
    w_out = din("w_out", [D, D])
    w_gate = din("w_gate", [D, c.DFF])
    w_up = din("w_up", [D, c.DFF])
    w_down = din("w_down", [c.DFF, D])
    y = nc.dram_tensor("y", [c.HALF, D], F32, kind="ExternalOutput").ap()

    def dscr(name, shape, dt):
        return nc.dram_tensor(name, list(shape), dt, kind="Internal").ap()

    kT = dscr("kT", [NHC, P, SEQ], BF16)
    qT = dscr("qT", [NHC, P, SEQ], BF16)
    vv = dscr("vv", [SEQ, AW], BF16)
    mixT = dscr("mixT", [2 * NHC, P, SEQ], BF16)
    hbuf = dscr("hbuf", [c.NOWN, D], F32)

    w_in_v = w_in.rearrange("(c p) f -> p c f", p=P)
    w_out_v = w_out.rearrange("(c p) f -> p c f", p=P)
    w_gate_v = w_gate.rearrange("(c p) f -> p c f", p=P)
    w_up_v = w_up.rearrange("(c p) f -> p c f", p=P)
    w_down_v = w_down.rearrange("(c p) f -> p c f", p=P)
    mixT_v = mixT.rearrange("c p s -> p c s")

    with ExitStack() as es:
        kb = KB(nc, es)
        op, dma, pe_group = kb.op, kb.dma, kb.pe_group
        V, S, G, T = nc.vector, nc.scalar, nc.gpsimd, nc.tensor

        bankT = es.enter_context(nc.psum_tensor("bankT", [P, 8, 512], F32))
        banks = [kb.view(bankT[:, i, :], "bank%d" % i) for i in range(8)]
        tp = [kb.view(bankT[:, 6 + i, 0:256].bitcast(BF16), "tp%d" % i) for i in range(2)]
        warena = es.enter_context(nc.sbuf_tensor("warena", [P, 32768], BF16))
        consts = kb.sb(es, [P, 3 * P], BF16, "consts")
        ident = consts.t[:, 0:P]
        tri = consts.t[:, P:2 * P]
        perm = consts.t[0:32, 2 * P:2 * P + 32]
        with ExitStack() as tmps:
            cstf = kb.sb(tmps, [P, 3 * P], F32, "cstf")
            dma("sp", cstf.t[:], cst[:, :], cstf, writes=[cstf])
            op("dve", lambda: V.tensor_copy(out=consts.t[:], in_=cstf.t[:]), reads=[cstf], writes=[consts])
            kb.barrier()
        NX = {}
        st4 = [kb.sb(es, [P, 4], F32, "st4") for _ in range(2)]
        nctr = [0]

        def load_grep(src):
            grep = NX["grep"]
            dma("sp", grep.t[:], src[:, :], grep, writes=[grep])

        def norm_load(src_ap):
            i = nctr[0]
            nctr[0] += 1
            xb, s4 = NX["xblk"][i % len(NX["xblk"])], st4[i % 2]
            dma("sp", xb.t[:], src_ap, xb, writes=[xb])
            return xb, s4

        def norm_block(src_ap, eps=1e-5, loaded=None):
            xn = NX["xn"]
            xb, s4 = loaded if loaded is not None else norm_load(src_ap)
            op("act", lambda: S.activation(out=xn.t[:], in_=xb.t[:], func=AF.Square, accum_out=s4.t[:, 0:1]),
               reads=[xb], writes=[xn, s4])
            op("dve", lambda: V.tensor_scalar(out=s4.t[:, 1:2], in0=s4.t[:, 0:1], scalar1=1.0 / D, scalar2=eps,
                                              op0=ALU.mult, op1=ALU.add), writes=[s4])
            op("act", lambda: S.activation(out=s4.t[:, 2:3], in_=s4.t[:, 1:2], func=AF.Sqrt), writes=[s4])
            op("dve", lambda: V.reciprocal(out=s4.t[:, 3:4], in_=s4.t[:, 2:3]), writes=[s4])
            return xb, s4

        def norm_to_T(src_ap, dstbuf, dst_of_dc4):
            norm_pre(src_ap)
            norm_post(dstbuf, dst_of_dc4)

        def norm_pre(src_ap, loaded=None):
            xb, s4 = norm_block(src_ap, loaded=loaded)
            xn = NX["xn"]
            op("act", lambda: S.activation(out=xn.t[:], in_=xb.t[:], func=AF.Copy, scale=s4.t[:, 3:4]),
               reads=[xb, s4], writes=[xn])

        def norm_post(dstbuf, dst_of_dc4):
            xn, gcol = NX["xn"], NX["gcol"]
            for d4 in range(DC // 4):
                tb = tp[d4 % 2]
                pe_group([(lambda k=k: T.transpose(out=tb.t[:, k * P:(k + 1) * P],
                                                   in_=xn.t[:, (d4 * 4 + k) * P:(d4 * 4 + k + 1) * P],
                                                   identity=ident)) for k in range(4)],
                         reads=[xn, consts], writes=[tb])
                dst = dst_of_dc4(d4 * 4)
                src = tb.t[:].rearrange("p (a b) -> p a b", b=P)
                gb_ = gcol.t[:, d4 * 4:d4 * 4 + 4].unsqueeze(2).broadcast_to([P, 4, P])
                op("dve", lambda: V.tensor_tensor(out=dst, in0=src, in1=gb_, op=ALU.mult), reads=[tb, gcol], accs=[dstbuf])

        def wload(wbuf, ap3, src3):
            dma("pool", ap3, src3, wbuf, writes=[wbuf])

        with ExitStack() as ph:
            T1 = c.T1
            nT = [kb.sb(ph, [P, DC, T1], BF16, "nT")] * 2
            NX["xblk"] = [kb.sb(ph, [P, D], F32, "xblk") for _ in range(2)]
            NX["xn"] = kb.sb(ph, [P, D], BF16, "xn")
            NX["gcol"] = kb.sb(ph, [P, DC], F32, "gcol")
            dma("sp", NX["gcol"].t[:], g1col[:, :], NX["gcol"], writes=[NX["gcol"]])
            w256a = [kb.view(warena[:, i * 8192:i * 8192 + DC * 256].rearrange("p (c f) -> p c f", f=256), "w256a")
                     for i in range(4)]
            ropc = kb.sb(ph, [32, 2], F32, "ropc")
            dma("sp", ropc.t[:], ropec[:, :], ropc, writes=[ropc])
            posi = kb.sb(ph, [32, T1], I32, "posi")
            ang = kb.sb(ph, [32, 2, T1], F32, "ang")
            angk = kb.sb(ph, [32, 2, T1], F32, "angk")
            angi = kb.sb(ph, [32, 2, T1], I32, "angi")
            tabs = [kb.sb(ph, [32, 2, T1], F32, "tab") for _ in range(2)]
            qk_sb = [kb.sb(ph, [P, T1], BF16, "qk_sb") for _ in range(3)]
            rt1 = [kb.sb(ph, [32, T1], F32, "rt1")] * 2
            rt2 = [kb.sb(ph, [32, T1], F32, "rt2")] * 2
            v_sb = [kb.sb(ph, [P, 512], BF16, "v_sb") for _ in range(4)]
            usb = [kb.sb(ph, [P, 16 + T1], F32, "usb") for _ in range(2)]
            sA = kb.sb(ph, [P, 16 + T1], F32, "sA")
            sB = kb.sb(ph, [P, 16 + T1], F32, "sB")
            sT = kb.sb(ph, [P, T1], F32, "sT")
            uhalo = kb.sb(ph, [P, 4 * PGC, 16], F32, "uhalo")
            invc = [kb.sb(ph, [P, T1], F32, "invc") for _ in range(2)]
            pooledT = [kb.sb(ph, [P, PGC, T1], BF16, "pooledT")] * 2
            posb = [kb.sb(ph, [P, T1], BF16, "posb") for _ in range(2)]
            wpool = [kb.sb(ph, [P, PGC, c.PG], BF16, "wpool") for _ in range(2)]
            psc = kb.sb(ph, [P, 4 * PGC], F32, "psc")
            dma("sp", psc.t[:], pscol[:, :], psc, writes=[psc])
            op("pool", lambda: G.memset(uhalo.t[:], 0.0), writes=[uhalo])
            accb = banks[0:3]
            swb = banks[3]
            plb = banks[4]
            acc_i = [0]
            qk_i = [0]

            def next_acc():
                b = accb[acc_i[0] % 3]
                acc_i[0] += 1
                return b

            TWO_PI = 2.0 * math.pi
            C1 = 6.28125
            C2 = TWO_PI - C1

            def rope_tables(t):
                tb = tabs[t % 2]
                s0 = t * T1
                dma("sp", posi.t[:], posr[:, s0:s0 + T1], posi, writes=[posi])
                op("dve", lambda: V.tensor_copy(out=ang.t[:, 0], in_=posi.t[:]), reads=[posi], writes=[ang])
                op("dve", lambda: V.tensor_scalar(out=ang.t[:, 0], in0=ang.t[:, 0], scalar1=ropc.t[:, 0:1],
                                                  scalar2=None, op0=ALU.mult), reads=[ropc], writes=[ang])
                op("dve", lambda: V.tensor_scalar(out=ang.t[:, 1], in0=ang.t[:, 0], scalar1=math.pi / 2,
                                                  scalar2=None, op0=ALU.add), writes=[ang])
                op("dve", lambda: V.tensor_scalar(out=angk.t[:], in0=ang.t[:], scalar1=1.0 / TWO_PI, scalar2=0.5,
                                                  op0=ALU.mult, op1=ALU.add), reads=[ang], writes=[angk])
                op("dve", lambda: V.tensor_copy(out=angi.t[:], in_=angk.t[:]), reads=[angk], writes=[angi])
                op("dve", lambda: V.tensor_copy(out=angk.t[:], in_=angi.t[:]), reads=[angi], writes=[angk])
                op("dve", lambda: V.scalar_tensor_tensor(out=ang.t[:], in0=angk.t[:], scalar=-C1, in1=ang.t[:],
                                                         op0=ALU.mult, op1=ALU.add), reads=[angk], writes=[ang])
                op("dve", lambda: V.scalar_tensor_tensor(out=ang.t[:], in0=angk.t[:], scalar=-C2, in1=ang.t[:],
                                                         op0=ALU.mult, op1=ALU.add), reads=[angk], writes=[ang])
                op("dve", lambda: V.tensor_scalar(out=angk.t[:], in0=ang.t[:], scalar1=math.pi, scalar2=-TWO_PI,
                                                  op0=ALU.is_gt, op1=ALU.mult), reads=[ang], writes=[angk])
                op("dve", lambda: V.tensor_tensor(out=ang.t[:], in0=ang.t[:], in1=angk.t[:], op=ALU.add),
                   reads=[angk], writes=[ang])
                op("dve", lambda: V.tensor_scalar(out=angk.t[:], in0=ang.t[:], scalar1=-math.pi, scalar2=TWO_PI,
                                                  op0=ALU.is_lt, op1=ALU.mult), reads=[ang], writes=[angk])
                op("dve", lambda: V.tensor_tensor(out=ang.t[:], in0=ang.t[:], in1=angk.t[:], op=ALU.add),
                   reads=[angk], writes=[ang])
                op("dve", lambda: V.tensor_scalar(out=ang.t[:], in0=ang.t[:], scalar1=math.pi, scalar2=-math.pi,
                                                  op0=ALU.min, op1=ALU.max), writes=[ang])
                op("act", lambda: S.activation(out=tb.t[:], in_=ang.t[:], func=AF.Sin), reads=[ang], writes=[tb])
                op("dve", lambda: V.tensor_scalar(out=tb.t[:, 0], in0=tb.t[:, 0], scalar1=ropc.t[:, 1:2],
                                                  scalar2=None, op0=ALU.mult), reads=[ropc], writes=[tb])

            pre_state = {}

            def xrows(t, bi):
                s0 = t * T1
                return xs[s0 + bi * P:s0 + (bi + 1) * P, :]

            def prep_early(t):
                l0 = norm_load(xrows(t, 0))
                l1 = norm_load(xrows(t, 1))
                norm_pre(None, loaded=l0)
                rope_tables(t)
                pre_state[t] = l1

            def prep_tile(t):
                dst = nT[t % 2]
                nb = T1 // P
                early = t in pre_state
                loads = {}
                if early:
                    loads[1] = pre_state[t]
                for bi in range(nb):
                    if not (early and bi == 0):
                        norm_pre(xrows(t, bi), loaded=loads.get(bi))
                    norm_post(dst, lambda dc0, bi=bi: dst.t[:, dc0:dc0 + 4, bi * P:(bi + 1) * P])
                    if early and bi + 2 < nb:
                        loads[bi + 2] = norm_load(xrows(t, bi + 2))
                if not early:
                    rope_tables(t)

            wl = []
            for t in range(c.NT1):
                kinds = ["k", "v"] + (["q", "u"] if t >= c.OWN_T0 else [])
                for kind in kinds:
                    for j in range(AW // 256):
                        wl.append((t, kind, j))
            col0 = {"q": 0, "k": AW, "v": 2 * AW, "u": 3 * AW}

            def issue_w(i):
                t, kind, j = wl[i]
                f0 = col0[kind] + j * 256
                wload(w256a[i % 4], w256a[i % 4].t[:], w_in_v[:, :, f0:f0 + 256])

            pend_rope = []

            def flush_rope():
                while pend_rope:
                    pend_rope.pop(0)()

            def qk_chunk(t, kind, ch, wb, sub):
                s0 = t * T1
                acc = next_acc()
                n = nT[t % 2]
                pe_group([(lambda dc=dc: T.matmul(acc.t[:], lhsT=wb.t[:, dc, sub * P:(sub + 1) * P], rhs=n.t[:, dc, :],
                                                  start=(dc == 0), stop=(dc == DC - 1))) for dc in range(DC)],
                         reads=[wb, n], writes=[acc])
                qs = qk_sb[qk_i[0] % 3]
                qk_i[0] += 1
                scale = (128.0 ** -0.5) if kind == "q" else 1.0
                op("act", lambda: S.activation(out=qs.t[:], in_=acc.t[:], func=AF.Copy, scale=scale),
                   reads=[acc], writes=[qs])
                flush_rope()
                tb = tabs[t % 2]
                r1, r2 = rt1[ch % 2], rt2[ch % 2]
                dstT = qT if kind == "q" else kT

                def rope():
                    pe_group([lambda: T.matmul(swb.t[0:32, :], lhsT=perm, rhs=qs.t[0:32, :], start=True, stop=True)],
                             reads=[qs, consts], writes=[swb])
                    op("dve", lambda: V.tensor_tensor(out=r1.t[:], in0=swb.t[0:32, :], in1=tb.t[:, 0], op=ALU.mult),
                       reads=[swb, tb], writes=[r1])
                    op("dve", lambda: V.tensor_tensor(out=r2.t[:], in0=qs.t[0:32, :], in1=tb.t[:, 1], op=ALU.mult),
                       reads=[qs, tb], writes=[r2])
                    op("dve", lambda: V.tensor_tensor(out=qs.t[0:32, :], in0=r1.t[:], in1=r2.t[:], op=ALU.add),
                       reads=[r1, r2], writes=[qs])
                    dma("sp", dstT[ch, :, s0:s0 + T1], qs.t[:], qs, reads=[qs])
                pend_rope.append(rope)

            def v_tile(t, j, wb):
                s0 = t * T1
                n = nT[t % 2]
                for bi in range(T1 // P):
                    acc = next_acc()
                    pe_group([(lambda dc=dc: T.matmul(acc.t[:, 0:256], lhsT=n.t[:, dc, bi * P:(bi + 1) * P], rhs=wb.t[:, dc, :],
                                                      start=(dc == 0), stop=(dc == DC - 1))) for dc in range(DC)],
                             reads=[wb, n], writes=[acc])
                    vs = v_sb[bi]
                    if j % 2 == 0:
                        op("act", lambda: S.copy(out=vs.t[:, 0:256], in_=acc.t[:, 0:256]), reads=[acc], writes=[vs])
                    else:
                        op("act", lambda: S.copy(out=vs.t[:, 256:512], in_=acc.t[:, 0:256]), reads=[acc], accs=[vs])
                        j2 = j // 2
                        dma("sp", vv[s0 + bi * P:s0 + (bi + 1) * P, j2 * 512:(j2 + 1) * 512], vs.t[:], vs, reads=[vs])

            def u_chunk(t, cu, wb, sub):
                s0 = t * T1
                g = cu // PGC
                cc = cu % PGC
                acc = next_acc()
                n = nT[t % 2]
                wpl, ivc = wpool[g % 2], invc[g % 2]
                if cc == 0:
                    dma("pool", wpl.t[:], w_pool[g].rearrange("(c p) d -> p c d", p=P), wpl, writes=[wpl])
                    o0 = t * T1 - c.OWN_S0
                    dma("sp", ivc.t[:], invcnt[:, g, o0:o0 + T1], ivc, writes=[ivc])
                pe_group([(lambda dc=dc: T.matmul(acc.t[:], lhsT=wb.t[:, dc, sub * P:(sub + 1) * P], rhs=n.t[:, dc, :],
                                                  start=(dc == 0), stop=(dc == DC - 1))) for dc in range(DC)],
                         reads=[wb, n], writes=[acc])
                ub = usb[cu % 2]
                op("act", lambda: S.copy(out=ub.t[:, 16:16 + T1], in_=acc.t[:]), reads=[acc], writes=[ub])
                op("pool", lambda: G.tensor_copy(out=ub.t[:, 0:16], in_=uhalo.t[:, cu, :]), reads=[uhalo], accs=[ub])
                op("pool", lambda: G.tensor_copy(out=uhalo.t[:, cu, :], in_=ub.t[:, T1:T1 + 16]), reads=[ub], writes=[uhalo])
                W = 16 + T1
                cur = ub
                for step, (sh, dst) in enumerate([(1, sA), (2, sB), (4, sA), (8, sB)][:g + 1]):
                    lo = 2 * sh - 1
                    src = cur
                    op("pool", lambda src=src, dst=dst, sh=sh, lo=lo: G.tensor_tensor(
                        out=dst.t[:, lo:W], in0=src.t[:, lo:W], in1=src.t[:, lo - sh:W - sh], op=ALU.add),
                       reads=[src], writes=[dst])
                    cur = dst
                pb = pooledT[g % 2]
                op("dve", lambda: V.tensor_tensor(out=sT.t[:], in0=cur.t[:, 16:W], in1=ivc.t[:], op=ALU.mult),
                   reads=[cur, ivc], writes=[sT])
                wr = dict(writes=[pb]) if cc == 0 else dict(accs=[pb])
                op("dve", lambda: V.tensor_tensor(out=pb.t[:, cc, :], in0=sT.t[:], in1=ub.t[:, 16:W], op=ALU.subtract),
                   reads=[sT, ub], **wr)
                if cc == PGC - 1:
                    for do in range(PGC):
                        pe_group([(lambda k=k: T.matmul(plb.t[:], lhsT=wpl.t[:, k, do * P:(do + 1) * P],
                                                        rhs=pb.t[:, k, :], start=(k == 0), stop=(k == PGC - 1)))
                                  for k in range(PGC)], reads=[wpl, pb], writes=[plb])
                        po = posb[do % 2]
                        col = g * PGC + do
                        op("act", lambda: S.activation(out=po.t[:], in_=plb.t[:], func=AF.Copy,
                                                       scale=psc.t[:, col:col + 1]), reads=[plb, psc], writes=[po])
                        dma("sp", mixT[NHC + col, :, s0:s0 + T1], po.t[:], po, reads=[po])

            prep_tile(0)
            for i0 in range(min(3, len(wl))):
                issue_w(i0)
            first_of_tile = {}
            for i, (t, kind, j) in enumerate(wl):
                first_of_tile.setdefault(t, i)
            for i, (t, kind, j) in enumerate(wl):
                if i + 3 < len(wl):
                    issue_w(i + 3)
                if i == first_of_tile[t] and t > 0:
                    flush_rope()
                    prep_tile(t)
                wb = w256a[i % 4]
                if kind in ("q", "k"):
                    for sub in range(2):
                        qk_chunk(t, kind, j * 2 + sub, wb, sub)
                elif kind == "v":
                    v_tile(t, j, wb)
                else:
                    for sub in range(2):
                        u_chunk(t, j * 2 + sub, wb, sub)
            flush_rope()
            kb.barrier()

        with ExitStack() as ph:
            NQB, QB0 = c.NQB, c.QB0
            Kt = [[kb.sb(ph, [P, SEQ], BF16, "Kt") for _ in range(2)] for _ in range(2)]
            Qt = [[kb.sb(ph, [P, NQB * P], BF16, "Qt") for _ in range(2)] for _ in range(2)]
            Vx = [kb.sb(ph, [P, NB, 258], BF16, "Vx") for _ in range(2)]
            kvf = kb.sb(ph, [P, NB], F32, "kvf")
            dma("sp", kvf.t[:], kvalid[:, :], kvf, writes=[kvf])
            pT = [kb.sb(ph, [P, 512], BF16, "pT") for _ in range(3)]
            o0b = kb.sb(ph, [P, 4, 256], F32, "o0b")
            ob = kb.sb(ph, [P, 4, 256], F32, "ob")
            onb = kb.sb(ph, [P, 4, 256], BF16, "onb")
            junk = kb.sb(ph, [P, 256], BF16, "junk")
            sm = kb.sb(ph, [P, 8, 4], F32, "sm")
            slc = kb.sb(ph, [P, 2], F32, "slc")
            aT = [kb.sb(ph, [P, 2, 512], BF16, "aT") for _ in range(2)]
            lamb = kb.sb(ph, [P, 4, P], F32, "lamb")
            lw = kb.sb(ph, [P, 8], F32, "lw")
            dma("sp", lamb.t[:], lamrep[:, :, :], lamb, writes=[lamb])
            dma("sp", slc.t[:], slncol[:, :], slc, writes=[slc])
            lprod = kb.sb(ph, [P, 2, P], F32, "lprod")
            op("dve", lambda: V.tensor_tensor(out=lprod.t[:, 0], in0=lamb.t[:, 0], in1=lamb.t[:, 1], op=ALU.mult),
               reads=[lamb], writes=[lprod])
            op("dve", lambda: V.tensor_tensor(out=lprod.t[:, 1], in0=lamb.t[:, 2], in1=lamb.t[:, 3], op=ALU.mult),
               reads=[lamb], writes=[lprod])
            op("dve", lambda: V.tensor_reduce(out=lw.t[:, 0:2], in_=lprod.t[:], axis=AX.X, op=ALU.add),
               reads=[lprod], writes=[lw])
            op("act", lambda: S.activation(out=lw.t[:, 2:4], in_=lw.t[:, 0:2], func=AF.Exp), writes=[lw])
            lambda_init = 0.8 - 0.6 * math.exp(-0.3 * 0)
            op("dve", lambda: V.tensor_tensor(out=lw.t[:, 4:5], in0=lw.t[:, 3:4], in1=lw.t[:, 2:3], op=ALU.subtract),
               writes=[lw])
            op("dve", lambda: V.tensor_scalar(out=lw.t[:, 4:5], in0=lw.t[:, 4:5], scalar1=-lambda_init, scalar2=None,
                                              op0=ALU.add), writes=[lw])
            op("dve", lambda: V.tensor_scalar(out=slc.t[:], in0=slc.t[:], scalar1=1.0 - lambda_init, scalar2=None,
                                              op0=ALU.mult), writes=[slc])
            sbk = banks[0:2]
            acb = banks[2:6]
            groups = [[QB0]] + [list(range(q, min(q + 4, NB))) for q in range(QB0 + 1, NB, 4)]

            def load_head(h):
                hb = h % 2
                for cc in range(2):
                    dma("sp", Kt[hb][cc].t[:], kT[h * 2 + cc, :, :], Kt[hb][cc], writes=[Kt[hb][cc]])
                    dma("sp", Qt[hb][cc].t[:], qT[h * 2 + cc, :, c.H0:SEQ], Qt[hb][cc], writes=[Qt[hb][cc]])
                dma("sp", Vx[hb].t[:, :, 0:256], vv.rearrange("(b p) e -> p b e", p=P)[:, :, h * 256:(h + 1) * 256],
                    Vx[hb], writes=[Vx[hb]])
                op("pool", lambda: G.tensor_copy(out=Vx[hb].t[:, :, 256:257], in_=kvf.t[:].unsqueeze(2)),
                   reads=[kvf], accs=[Vx[hb]])

            sidx = [0]
            eidx = [0]
            load_head(0)
            for h in range(NH):
                hb = h % 2
                if h + 1 < NH:
                    load_head(h + 1)
                for grp in groups:
                    g0, gl = grp[0], grp[-1]
                    for cc in range(2):
                        K, Q, Vb = Kt[hb][cc], Qt[hb][cc], Vx[hb]
                        pend = []

                        def s_step(kbk):
                            fv = max(g0, kbk)
                            n = (gl - fv + 1) * P
                            i = sidx[0]
                            sidx[0] += 1
                            sb_, pb_ = sbk[i % 2], pT[i % 3]
                            pe_group([lambda: T.matmul(sb_.t[:, 0:n], lhsT=K.t[:, kbk * P:(kbk + 1) * P],
                                                       rhs=Q.t[:, (fv - QB0) * P:(gl + 1 - QB0) * P], start=True, stop=True)],
                                     reads=[K, Q], writes=[sb_])
                            op("act", lambda: S.activation(out=pb_.t[:, 0:n], in_=sb_.t[:, 0:n], func=AF.Exp),
                               reads=[sb_], writes=[pb_])
                            if kbk >= g0:
                                op("pool", lambda: G.tensor_tensor(out=pb_.t[:, 0:P], in0=pb_.t[:, 0:P], in1=tri, op=ALU.mult),
                                   reads=[consts], writes=[pb_])
                            return (kbk, fv, pb_)

                        def pv_step(item):
                            kbk, fv, pb_ = item
                            fns = [(lambda qb=qb: T.matmul(acb[qb - g0].t[:, 0:257],
                                                           lhsT=pb_.t[:, (qb - fv) * P:(qb - fv + 1) * P],
                                                           rhs=Vb.t[:, kbk, 0:257], start=(kbk == 0), stop=(kbk == qb)))
                                   for qb in range(fv, gl + 1)]
                            accl = [acb[qb - g0] for qb in range(fv, gl + 1)]
                            if kbk == 0:
                                pe_group(fns, reads=[pb_, Vb], writes=accl)
                            else:
                                pe_group(fns, reads=[pb_, Vb], accs=accl)

                        for kbk in range(gl + 1):
                            pend.append(s_step(kbk))
                            if len(pend) > 1:
                                pv_step(pend.pop(0))
                        while pend:
                            pv_step(pend.pop(0))
                        nq = len(grp)
                        accs_ = acb[0:nq]
                        a4 = bankT[:, 2:2 + nq, :]
                        st = sm.t
                        op("dve", lambda: V.tensor_scalar(out=st[:, 0, 0:nq].unsqueeze(2), in0=a4[:, :, 256:257], scalar1=1e-20,
                                                          scalar2=None, op0=ALU.max), reads=accs_, writes=[sm])
                        op("dve", lambda: V.reciprocal(out=st[:, 1, 0:nq], in_=st[:, 0, 0:nq]), writes=[sm])
                        if cc == 0:
                            op("dve", lambda: V.tensor_tensor(out=o0b.t[:, 0:nq, :], in0=a4[:, :, 0:256],
                                                              in1=st[:, 1, 0:nq].unsqueeze(2).broadcast_to([P, nq, 256]),
                                                              op=ALU.mult), reads=accs_ + [sm], writes=[o0b])
                        else:
                            op("dve", lambda: V.tensor_scalar(out=st[:, 2, 0:nq], in0=st[:, 1, 0:nq], scalar1=lw.t[:, 4:5],
                                                              scalar2=None, op0=ALU.mult), reads=[lw], writes=[sm])
                            op("dve", lambda: V.tensor_tensor(out=ob.t[:, 0:nq, :], in0=a4[:, :, 0:256],
                                                              in1=st[:, 2, 0:nq].unsqueeze(2).broadcast_to([P, nq, 256]),
                                                              op=ALU.mult), reads=accs_ + [sm], writes=[ob])
                            op("pool", lambda: G.tensor_tensor(out=ob.t[:, 0:nq, :], in0=ob.t[:, 0:nq, :], in1=o0b.t[:, 0:nq, :],
                                                               op=ALU.add), reads=[o0b], writes=[ob])
                            for qi in range(nq):
                                op("act", lambda: S.activation(out=junk.t[:], in_=ob.t[:, qi, :], func=AF.Square,
                                                               accum_out=st[:, 3, qi:qi + 1]), reads=[ob], writes=[junk], accs=[sm])
                            op("dve", lambda: V.tensor_scalar(out=st[:, 4, 0:nq], in0=st[:, 3, 0:nq], scalar1=1.0 / 256,
                                                              scalar2=1e-5, op0=ALU.mult, op1=ALU.add), writes=[sm])
                            op("act", lambda: S.activation(out=st[:, 5, 0:nq], in_=st[:, 4, 0:nq], func=AF.Sqrt), writes=[sm])
                            op("dve", lambda: V.reciprocal(out=st[:, 6, 0:nq], in_=st[:, 5, 0:nq]), writes=[sm])
                            op("dve", lambda: V.tensor_tensor(out=onb.t[:, 0:nq, :], in0=ob.t[:, 0:nq, :],
                                                              in1=st[:, 6, 0:nq].unsqueeze(2).broadcast_to([P, nq, 256]),
                                                              op=ALU.mult), reads=[ob, sm], writes=[onb])
                            for ec in range(2):
                                pe_group([(lambda qi=qi: T.transpose(out=tp[ec].t[:, qi * P:(qi + 1) * P],
                                                                     in_=onb.t[:, qi, ec * P:(ec + 1) * P], identity=ident))
                                          for qi in range(nq)], reads=[onb, consts], writes=[tp[ec]])
                            ng = nq * P
                            ab = aT[eidx[0] % 2]
                            eidx[0] += 1
                            op("dve", lambda: V.tensor_scalar(out=ab.t[:, 0, 0:ng], in0=tp[0].t[:, 0:ng], scalar1=slc.t[:, 0:1],
                                                              scalar2=None, op0=ALU.mult), reads=[tp[0], slc], writes=[ab])
                            op("act", lambda: S.activation(out=ab.t[:, 1, 0:ng], in_=tp[1].t[:, 0:ng], func=AF.Copy,
                                                           scale=slc.t[:, 1:2]), reads=[tp[1], slc], accs=[ab])
                            dma("sp", mixT_v[:, h * 2:h * 2 + 2, g0 * P:g0 * P + ng], ab.t[:, :, 0:ng], ab, reads=[ab])
            kb.barrier()

        with ExitStack() as ph:
            n2T = kb.sb(ph, [P, DC, 512], BF16, "n2T")
            n2h = kb.sb(ph, [P, DC, 2], BF16, "n2h")
            big = ph.enter_context(nc.sbuf_tensor("big", [P, max(FC * 512, 10 * D, DC * 640)], BF16))
            NX["grep"] = kb.view(big[:, 2 * D:4 * D].bitcast(F32), "grep3")
            NX["xblk"] = [kb.view(big[:, (4 + 2 * i) * D:(6 + 2 * i) * D].bitcast(F32), "xblk3") for i in range(2)]
            NX["xn"] = kb.view(big[:, 8 * D:9 * D], "xn3")
            NX["gcol"] = kb.sb(ph, [P, DC], F32, "gcol3")
            dma("sp", NX["gcol"].t[:], g2col[:, :], NX["gcol"], writes=[NX["gcol"]])
            actT = kb.view(big[:, 0:FC * 512].rearrange("p (c t) -> p c t", t=512), "actT")
            gcarry = kb.sb(ph, [P, FC, 2], F32, "gcarry")
            cwc = kb.sb(ph, [P, FC, 3], F32, "cwc")
            cbc = kb.sb(ph, [P, FC], F32, "cbc")
            dma("sp", cwc.t[:], cwcol[:, :, :], cwc, writes=[cwc])
            dma("sp", cbc.t[:], cbcol[:, :], cbc, writes=[cbc])
            sl_in4 = [kb.sb(ph, [P, 512], F32, "sl_in") for _ in range(4)]
            sl_in = sl_in4[0:2]
            sl_out = [kb.sb(ph, [P, 512], F32, "sl_out") for _ in range(2)]
            gs = [kb.sb(ph, [P, 2 + 512], F32, "gs") for _ in range(2)]
            gh = kb.sb(ph, [2, 256], BF16, "gh")
            cv = [kb.sb(ph, [P, 512], F32, "cv") for _ in range(2)]
            sil = [kb.sb(ph, [P, 512], BF16, "sil") for _ in range(2)]
            sli = [0]

            mt = kb.view(big[:, 0:DC * 640].rearrange("p (c t) -> p c t", t=640), "mt")
            w512o = [kb.view(warena[:, i * 16384:i * 16384 + DC * 512].rearrange("p (c f) -> p c f", f=512), "w512o")
                     for i in range(2)]
            w256 = [kb.view(warena[:, i * 8192:i * 8192 + DC * 256].rearrange("p (c f) -> p c f", f=256), "w256")
                    for i in range(4)]
            w512d = [kb.view(warena[:, i * 16384:(i + 1) * 16384].rearrange("p (c f) -> p c f", f=512), "w512d")
                     for i in range(2)]
            NXe = {"grep": kb.view(big[:, 4 * D:6 * D].bitcast(F32), "grepe"),
                   "xblk": [kb.view(big[:, (6 + 2 * i) * D:(8 + 2 * i) * D].bitcast(F32), "xblke") for i in range(2)],
                   "xn": kb.view(n2T.t[:, 0:D // 512, :].rearrange("p a b -> p (a b)"), "xne")}
            ybs = [kb.view(big[:, 2 * i * D:(2 * i + 2) * D].bitcast(F32), "yb") for i in range(2)]
            a_pref = [False]
            for ti in range(c.NMAIN):
                m0 = c.HALF + ti * 512
                blocks = ([c.H0] if ti == 0 else []) + [m0 + k * P for k in range(4)]
                TW = len(blocks) * P
                s0 = blocks[0]
                dma("sp", mt.t[:, :, 0:TW], mixT_v[:, :, s0:s0 + TW], mt, writes=[mt])
                w512 = w512o
                nsl = D // 512
                if not a_pref[0]:
                    wload(w512[0], w512[0].t[:], w_out_v[:, :, 0:512])
                ai = 0
                for s in range(nsl):
                    if s + 1 < nsl and not (a_pref[0] and s == 0):
                        wload(w512[(s + 1) % 2], w512[(s + 1) % 2].t[:], w_out_v[:, :, (s + 1) * 512:(s + 2) * 512])
                    wb = w512[s % 2]
                    for bi, bs in enumerate(blocks):
                        acc = banks[ai % 3]
                        ai += 1
                        k = sli[0] % 2
                        sli[0] += 1
                        xi, ho = sl_in[k], sl_out[k]
                        dma("sp", xi.t[:], xs[bs:bs + P, s * 512:(s + 1) * 512], xi, writes=[xi])
                        pe_group([(lambda fc=fc: T.matmul(acc.t[:], lhsT=mt.t[:, fc, bi * P:(bi + 1) * P], rhs=wb.t[:, fc, :],
                                                          start=(fc == 0), stop=(fc == DC - 1))) for fc in range(DC)],
                                 reads=[mt, wb], writes=[acc])
                        op("dve", lambda: V.tensor_tensor(out=ho.t[:], in0=acc.t[:], in1=xi.t[:], op=ALU.add),
                           reads=[acc, xi], writes=[ho])
                        r0 = bs - c.H0
                        dma("sp", hbuf[r0:r0 + P, s * 512:(s + 1) * 512], ho.t[:], ho, reads=[ho])
                kb.barrier()
                wgb, wub = w256[0:2], w256[2:4]
                nwt = (FC + 1) // 2

                def issue_gu(i):
                    f0 = i * 256
                    fw = min(256, c.DFF - f0)
                    wload(wgb[i % 2], wgb[i % 2].t[:, :, 0:fw], w_gate_v[:, :, f0:f0 + fw])
                    wload(wub[i % 2], wub[i % 2].t[:, :, 0:fw], w_up_v[:, :, f0:f0 + fw])
                issue_gu(0)
                if nwt > 1:
                    issue_gu(1)
                for bi, bs in enumerate(blocks):
                    r0 = bs - c.H0
                    if ti == 0 and bi == 0:
                        xb, s4 = norm_block(hbuf[r0:r0 + P, :])
                        xn, gcol = NX["xn"], NX["gcol"]
                        op("act", lambda: S.activation(out=xn.t[:], in_=xb.t[:], func=AF.Copy, scale=s4.t[:, 3:4]),
                           reads=[xb, s4], writes=[xn])
                        for d4 in range(DC // 4):
                            tb = tp[d4 % 2]
                            pe_group([(lambda k=k: T.transpose(out=tb.t[:, k * P:(k + 1) * P],
                                                               in_=xn.t[:, (d4 * 4 + k) * P:(d4 * 4 + k + 1) * P],
                                                               identity=ident)) for k in range(4)],
                                     reads=[xn, consts], writes=[tb])
                            op("dve", lambda: V.tensor_tensor(
                                out=n2h.t[:, d4 * 4:d4 * 4 + 4, :],
                                in0=tb.t[:].rearrange("p (a b) -> p a b", b=P)[:, :, P - 2:P],
                                in1=gcol.t[:, d4 * 4:d4 * 4 + 4].unsqueeze(2).broadcast_to([P, 4, 2]), op=ALU.mult),
                               reads=[tb, gcol], accs=[n2h])
                    else:
                        mb = bi - (1 if ti == 0 else 0)
                        norm_to_T(hbuf[r0:r0 + P, :], n2T,
                                  lambda dc0, mb=mb: n2T.t[:, dc0:dc0 + 4, mb * P:(mb + 1) * P])
                kb.barrier()
                for i in range(nwt):
                    if i == 0 and nwt > 1:
                        pass
                    elif i + 1 < nwt:
                        issue_gu(i + 1)
                    wg_, wu_ = wgb[i % 2], wub[i % 2]
                    nsub = min(2, FC - 2 * i)
                    if ti == 0:
                        hb_, fw_ = banks[4], nsub * P
                        pe_group([(lambda dc=dc: T.matmul(hb_.t[0:2, 0:fw_], lhsT=n2h.t[:, dc, :], rhs=wg_.t[:, dc, 0:fw_],
                                                          start=(dc == 0), stop=(dc == DC - 1)))
                                  for dc in range(DC)], reads=[wg_, n2h], writes=[hb_])
                        op("act", lambda: S.copy(out=gh.t[0:2, 0:fw_], in_=hb_.t[0:2, 0:fw_]), reads=[hb_], writes=[gh])
                    for sub in range(nsub):
                        j = 2 * i + sub
                        gb, ub_ = banks[j % 2], banks[2 + j % 2]
                        g_, cv_, sl_ = gs[j % 2], cv[j % 2], sil[j % 2]
                        if ti == 0:
                            if sub == 0:
                                pe_group([(lambda k=k: T.transpose(out=tp[0].t[:, 2 * k:2 * k + 2],
                                                                   in_=gh.t[0:2, k * P:(k + 1) * P],
                                                                   identity=consts.t[0:2, 0:2])) for k in range(nsub)],
                                         reads=[gh, consts], writes=[tp[0]])
                            op("act", lambda: S.copy(out=g_.t[:, 0:2], in_=tp[0].t[:, 2 * sub:2 * sub + 2]),
                               reads=[tp[0]], writes=[g_])
                        else:
                            op("act", lambda: S.copy(out=g_.t[:, 0:2], in_=gcarry.t[:, j, :]), reads=[gcarry], writes=[g_])
                        pe_group([(lambda dc=dc: T.matmul(gb.t[:], lhsT=wg_.t[:, dc, sub * P:(sub + 1) * P], rhs=n2T.t[:, dc, :],
                                                          start=(dc == 0), stop=(dc == DC - 1))) for dc in range(DC)],
                                 reads=[wg_, n2T], writes=[gb])
                        pe_group([(lambda dc=dc: T.matmul(ub_.t[:], lhsT=wu_.t[:, dc, sub * P:(sub + 1) * P], rhs=n2T.t[:, dc, :],
                                                          start=(dc == 0), stop=(dc == DC - 1))) for dc in range(DC)],
                                 reads=[wu_, n2T], writes=[ub_])
                        op("act", lambda: S.copy(out=g_.t[:, 2:514], in_=gb.t[:]), reads=[gb], accs=[g_])
                        op("pool", lambda: G.tensor_copy(out=gcarry.t[:, j, :], in_=g_.t[:, 512:514]), reads=[g_], accs=[gcarry])
                        op("dve", lambda: V.tensor_scalar(out=cv_.t[:], in0=g_.t[:, 2:514], scalar1=cwc.t[:, j, 2:3],
                                                          scalar2=cbc.t[:, j:j + 1], op0=ALU.mult, op1=ALU.add),
                           reads=[g_, cwc, cbc], writes=[cv_])
                        op("dve", lambda: V.scalar_tensor_tensor(out=cv_.t[:], in0=g_.t[:, 1:513], scalar=cwc.t[:, j, 1:2],
                                                                 in1=cv_.t[:], op0=ALU.mult, op1=ALU.add),
                           reads=[g_], writes=[cv_])
                        op("dve", lambda: V.scalar_tensor_tensor(out=cv_.t[:], in0=g_.t[:, 0:512], scalar=cwc.t[:, j, 0:1],
                                                                 in1=cv_.t[:], op0=ALU.mult, op1=ALU.add),
                           reads=[g_], writes=[cv_])
                        op("act", lambda: S.activation(out=sl_.t[:], in_=cv_.t[:], func=AF.Silu), reads=[cv_], writes=[sl_])
                        op("dve", lambda: V.tensor_tensor(out=actT.t[:, j, :], in0=sl_.t[:], in1=ub_.t[:], op=ALU.mult),
                           reads=[sl_, ub_], accs=[actT])
                kb.barrier()
                w512 = w512d
                fgs = [(f0, min(32, FC - f0)) for f0 in range(0, FC, 32)]
                dl = [(s, f0, nf) for s in range(nsl) for (f0, nf) in fgs]

                def issue_d(i):
                    s, f0, nf = dl[i]
                    wload(w512[i % 2], w512[i % 2].t[:, 0:nf, :], w_down_v[:, f0:f0 + nf, s * 512:(s + 1) * 512])
                issue_d(0)
                for i, (s, f0, nf) in enumerate(dl):
                    if i + 1 < len(dl):
                        issue_d(i + 1)
                    wb = w512[i % 2]
                    aset = banks[(s % 2) * 4:(s % 2) * 4 + 4]
                    if f0 == 0:
                        for bi in range(4):
                            r0 = m0 + bi * P - c.H0
                            dma("sp", sl_in4[bi].t[:], hbuf[r0:r0 + P, s * 512:(s + 1) * 512], sl_in4[bi], writes=[sl_in4[bi]])
                    for a in range(nf):
                        j = f0 + a
                        fns = [(lambda bi=bi: T.matmul(aset[bi].t[:], lhsT=actT.t[:, j, bi * P:(bi + 1) * P], rhs=wb.t[:, a, :],
                                                       start=(j == 0), stop=(j == FC - 1))) for bi in range(4)]
                        if j == 0:
                            pe_group(fns, reads=[actT, wb], writes=list(aset))
                        else:
                            pe_group(fns, reads=[actT, wb], accs=list(aset))
                    if f0 + nf == FC:
                        for bi in range(4):
                            acc = aset[bi]
                            k = sli[0] % 2
                            sli[0] += 1
                            xi, ho = sl_in4[bi], sl_out[k]
                            r0 = m0 + bi * P - c.H0
                            op("dve", lambda: V.tensor_tensor(out=ho.t[:], in0=acc.t[:], in1=xi.t[:], op=ALU.add),
                               reads=[acc, xi], writes=[ho])
                            dma("sp", hbuf[r0:r0 + P, s * 512:(s + 1) * 512], ho.t[:], ho, reads=[ho])
                kb.barrier()
                NXb = dict(NX)
                NX.update(NXe)
                load_grep(gfrep)
                a_pref[0] = False
                if ti + 1 < c.NMAIN:
                    for k_ in range(2):
                        wload(w512o[k_], w512o[k_].t[:], w_out_v[:, :, k_ * 512:(k_ + 1) * 512])
                    a_pref[0] = True
                grep = NX["grep"]
                for bi in range(4):
                    yb = ybs[bi % 2]
                    r0 = m0 + bi * P - c.H0
                    xb, s4 = norm_block(hbuf[r0:r0 + P, :])
                    op("dve", lambda: V.scalar_tensor_tensor(out=yb.t[:], in0=xb.t[:], scalar=s4.t[:, 3:4], in1=grep.t[:],
                                                             op0=ALU.mult, op1=ALU.mult),
                       reads=[xb, s4, grep], writes=[yb])
                    o_r = m0 + bi * P - c.HALF
                    dma("sp", y[o_r:o_r + P, :], yb.t[:], yb, reads=[yb])
                kb.barrier()
                NX.clear()
                NX.update(NXb)
    return nc


def host_inputs(cfg, inputs):
    c = cfg
    f32 = np.float32
    x = np.asarray(inputs["x"], f32)
    positions = np.asarray(inputs["positions"]).astype(np.int32)
    l = 0
    inv_freq = (1.0 / (500000.0 ** (np.arange(0, 32, 2, dtype=f32) / f32(32)))).astype(f32)
    ropec = np.zeros((32, 2), f32)
    ropec[:, 0] = np.concatenate([inv_freq, inv_freq])
    ropec[:16, 1] = -1.0
    ropec[16:, 1] = 1.0
    cst = np.zeros((P, 3 * P), f32)
    cst[:, 0:P] = np.eye(P, dtype=f32)
    kk, qq = np.meshgrid(np.arange(P), np.arange(P), indexing="ij")
    cst[:, P:2 * P] = (qq >= kk).astype(f32)
    for m in range(32):
        cst[(m + 16) % 32, 2 * P + m] = 1.0

    def rep(v, n=P):
        return np.ascontiguousarray(np.broadcast_to(np.asarray(v, f32).reshape(1, -1), (n, np.asarray(v).size)))

    def col(v, nchunk):
        return np.ascontiguousarray(np.asarray(v, f32).reshape(nchunk, P).T)

    shared = {
        "ropec": ropec, "cst": cst,
        "gfrep": rep(inputs["norm_f_g"]),
        "lamrep": np.ascontiguousarray(np.stack([rep(inputs[k][l]) for k in
                                                 ("lambda_q1", "lambda_k1", "lambda_q2", "lambda_k2")], axis=1)),
        "slncol": col(inputs["subln_g"][l], 2),
        "g1col": col(inputs["norm1_g"][l], c.DC), "g2col": col(inputs["norm2_g"][l], c.DC),
        "pscol": col(inputs["pool_scale"][l], 4 * c.PGC),
        "cwcol": np.ascontiguousarray(np.asarray(inputs["conv_w"][l], f32).reshape(3, c.FC, P).transpose(2, 1, 0)),
        "cbcol": col(inputs["conv_b"][l], c.FC),
        "w_in": np.ascontiguousarray(inputs["w_in"][l], f32), "w_pool": np.ascontiguousarray(inputs["w_pool"][l], f32),
        "w_out": np.ascontiguousarray(inputs["w_out"][l], f32), "w_gate": np.ascontiguousarray(inputs["w_gate"][l], f32),
        "w_up": np.ascontiguousarray(inputs["w_up"][l], f32), "w_down": np.ascontiguousarray(inputs["w_down"][l], f32),
    }
    in_maps = []
    for b in range(c.BATCH):
        for half in range(2):
            seqpos = np.arange(c.SEQ) - (0 if half == 1 else c.HALF)
            valid = seqpos >= 0
            xs_ = np.zeros((c.SEQ, c.D), f32)
            pos = np.zeros((c.SEQ,), np.int32)
            if half == 1:
                xs_[:] = x[b]
                pos[:] = positions[b]
            else:
                xs_[c.HALF:] = x[b, :c.HALF]
                pos[c.HALF:] = positions[b, :c.HALF]
            kv = valid.astype(f32).reshape(c.NB, P).T
            own = seqpos[c.OWN_S0:]
            ic = np.ones((4, own.size), f32)
            for g, w in enumerate((2, 4, 8, 16)):
                cnt = np.minimum(np.maximum(own + 1, 1), w).astype(f32)
                ic[g] = f32(1.0) / cnt
            m = dict(shared)
            m["xs"] = xs_
            m["posr"] = np.ascontiguousarray(np.broadcast_to(pos[None, :], (32, c.SEQ)))
            m["kvalid"] = np.ascontiguousarray(kv)
            m["invcnt"] = np.ascontiguousarray(np.broadcast_to(ic[None], (P, 4, own.size)))
            in_maps.append(m)
    return in_maps


_NC_CACHE = {}


def run(cfg, inputs, trace=False):
    key = (cfg.D, cfg.SEQ, cfg.DFF, cfg.BATCH)
    if key not in _NC_CACHE:
        _NC_CACHE[key] = build(cfg)
    nc = _NC_CACHE[key]
    in_maps = host_inputs(cfg, inputs)
    ncores = len(in_maps)
    res = run_bass_kernel_spmd(nc, in_maps, core_ids=list(range(ncores)), trace=trace)
    out = np.zeros((cfg.BATCH, cfg.SEQ, cfg.D), np.float32)
    for b in range(cfg.BATCH):
        for half in range(2):
            out[b, half * cfg.HALF:(half + 1) * cfg.HALF] = res.results[b * 2 + half]["y"]
    return out, res


def kernel(**inputs):
    cfg = Cfg()
    out, _ = run(cfg, inputs)
    return out
```

```python
import math
from contextlib import ExitStack

import numpy as np
import concourse.bass as bass
import concourse.mybir as mybir
from concourse.bass_utils import run_bass_kernel_spmd

F32 = mybir.dt.float32
BF16 = mybir.dt.bfloat16
I32 = mybir.dt.int32
AF = mybir.ActivationFunctionType
ALU = mybir.AluOpType
AX = mybir.AxisListType

P = 128


class Cfg:
    def __init__(self, D=4096, SEQ=4096, DFF=11008, BATCH=4):
        self.D, self.SEQ, self.DFF, self.BATCH = D, SEQ, DFF, BATCH
        self.DC = D // P
        self.AW = D // 2
        self.NH = self.AW // 256
        self.NHC = self.NH * 2
        self.PW = D - self.AW
        self.PG = self.PW // 4
        self.PGC = self.PG // P
        self.INW = 3 * self.AW + self.PW
        self.FC = DFF // P
        self.NB = SEQ // P
        self.HALF = SEQ // 2
        self.T1 = 512
        self.NT1 = SEQ // self.T1
        self.OWN_T0 = (self.HALF - P) // self.T1
        self.OWN_S0 = self.OWN_T0 * self.T1
        self.QB0 = self.HALF // P - 1
        self.NQB = self.NB - self.QB0
        self.H0 = self.HALF - P
        self.NOWN = self.HALF + P
        self.NMAIN = self.HALF // 512


class Buf:
    __slots__ = ("t", "w", "r", "pr", "sem", "cnt", "name")

    def __init__(self, t, name):
        self.t, self.name = t, name
        self.w, self.r, self.pr = {}, {}, {}
        self.sem, self.cnt = None, 0


class KB:
    def __init__(self, nc, es):
        self.nc, self.es = nc, es
        self.E = {"pe": nc.tensor, "act": nc.scalar, "dve": nc.vector, "pool": nc.gpsimd, "sp": nc.sync}
        self.sem = {k: es.enter_context(nc.semaphore("e_" + k)) for k in ("pe", "act", "dve", "pool")}
        self.cnt = {k: 0 for k in self.sem}
        self.waited = {k: {} for k in self.E}
        self.owners = []
        self.nbuf = 0

    def sb(self, stack, shape, dtype, name):
        self.nbuf += 1
        nm = "%s_%d" % (name, self.nbuf)
        return Buf(stack.enter_context(self.nc.sbuf_tensor(nm, list(shape), dtype)), nm)

    def ps(self, stack, shape, dtype, name):
        self.nbuf += 1
        nm = "%s_%d" % (name, self.nbuf)
        return Buf(stack.enter_context(self.nc.psum_tensor(nm, list(shape), dtype)), nm)

    def view(self, ap, name):
        self.nbuf += 1
        return Buf(ap, "%s_%d" % (name, self.nbuf))

    def _wait(self, e, deps):
        for key, (sem, val) in deps.items():
            if self.waited[e].get(key, 0) < val:
                self.E[e].wait_ge(sem, val)
                self.waited[e][key] = val

    @staticmethod
    def _deps(reads, writes, accs):
        d = {}

        def add(dic):
            for k, (sem, v) in dic.items():
                if k not in d or d[k][1] < v:
                    d[k] = (sem, v)
        for b in reads:
            add(b.w)
        for b in writes:
            add(b.w)
            add(b.r)
        for b in accs:
            add(b.r)
            add(b.pr)
        return d

    @staticmethod
    def _record(ev, reads, writes, accs):
        k, sem, v = ev
        for b in writes:
            b.w = {k: (sem, v)}
            b.pr = b.r
            b.r = {}
        for b in accs:
            if k not in b.w or b.w[k][1] < v:
                b.w[k] = (sem, v)
        for b in reads:
            if k not in b.r or b.r[k][1] < v:
                b.r[k] = (sem, v)

    def op(self, e, fn, reads=(), writes=(), accs=()):
        self._wait(e, self._deps(reads, writes, accs))
        ins = fn()
        self.cnt[e] += 1
        ins.then_inc(self.sem[e], 1)
        self._record((e, self.sem[e], self.cnt[e]), reads, writes, accs)
        return ins

    def pe_group(self, fns, reads=(), writes=(), accs=()):
        self._wait("pe", self._deps(reads, writes, accs))
        ins = None
        for fn in fns:
            ins = fn()
        self.cnt["pe"] += 1
        ins.then_inc(self.sem["pe"], 1)
        self._record(("pe", self.sem["pe"], self.cnt["pe"]), reads, writes, accs)

    def dma(self, q, out, in_, owner, reads=(), writes=(), accs=()):
        self._wait(q, self._deps(reads, writes, accs))
        if owner.sem is None:
            owner.sem = self.es.enter_context(self.nc.semaphore("d_" + owner.name))
            self.owners.append(owner)
        owner.cnt += 16
        self.E[q].dma_start(out=out, in_=in_).then_inc(owner.sem, 16)
        self._record(("d_" + owner.name, owner.sem, owner.cnt), reads, writes, accs)

    def barrier(self):
        ev = {k: (self.sem[k], self.cnt[k]) for k in self.sem if self.cnt[k] > 0}
        for b in self.owners:
            ev["d_" + b.name] = (b.sem, b.cnt)
        for e in self.E:
            self._wait(e, ev)


def build(cfg):
    c = cfg
    D, DC, SEQ, AW, NH, NHC, PGC, FC, NB = c.D, c.DC, c.SEQ, c.AW, c.NH, c.NHC, c.PGC, c.FC, c.NB
    nc = bass.Bass("TRN2", target_bir_lowering=False)

    def din(name, shape, dt=F32):
        return nc.dram_tensor(name, list(shape), dt, kind="ExternalInput").ap()

    xs = din("xs", [SEQ, D])
    posr = din("posr", [32, SEQ], I32)
    ropec = din("ropec", [32, 2])
    kvalid = din("kvalid", [P, NB])
    invcnt = din("invcnt", [P, 4, SEQ - c.OWN_S0])
    cst = din("cst", [P, 3 * P])
    gfrep = din("gfrep", [P, D])
    lamrep = din("lamrep", [P, 4, P])
    slncol = din("slncol", [P, 2])
    g1col = din("g1col", [P, DC])
    g2col = din("g2col", [P, DC])
    pscol = din("pscol", [P, 4 * PGC])
    cwcol = din("cwcol", [P, FC, 3])
    cbcol = din("cbcol", [P, FC])
    w_in = din("w_in", [D, c.INW])
    w_pool = din("w_pool", [4, c.PG, c.PG])
    w_out = din("w_out", [D, D])
    w_gate = din("w_gate", [D, c.DFF])
    w_up = din("w_up", [D, c.DFF])
    w_down = din("w_down", [c.DFF, D])
    y = nc.dram_tensor("y", [c.HALF, D], F32, kind="ExternalOutput").ap()

    def dscr(name, shape, dt):
        return nc.dram_tensor(name, list(shape), dt, kind="Internal").ap()

    kT = dscr("kT", [NHC, P, SEQ], BF16)
    qT = dscr("qT", [NHC, P, SEQ], BF16)
    vv = dscr("vv", [SEQ, AW], BF16)
    mixT = dscr("mixT", [2 * NHC, P, SEQ], BF16)
    hbuf = dscr("hbuf", [c.NOWN, D], F32)

    w_in_v = w_in.rearrange("(c p) f -> p c f", p=P)
    w_out_v = w_out.rearrange("(c p) f -> p c f", p=P)
    w_gate_v = w_gate.rearrange("(c p) f -> p c f", p=P)
    w_up_v = w_up.rearrange("(c p) f -> p c f", p=P)
    w_down_v = w_down.rearrange("(c p) f -> p c f", p=P)
    mixT_v = mixT.rearrange("c p s -> p c s")

    with ExitStack() as es:
        kb = KB(nc, es)
        op, dma, pe_group = kb.op, kb.dma, kb.pe_group
        V, S, G, T = nc.vector, nc.scalar, nc.gpsimd, nc.tensor

        bankT = es.enter_context(nc.psum_tensor("bankT", [P, 8, 512], F32))
        banks = [kb.view(bankT[:, i, :], "bank%d" % i) for i in range(8)]
        tp = [kb.view(bankT[:, 6 + i, 0:256].bitcast(BF16), "tp%d" % i) for i in range(2)]
        warena = es.enter_context(nc.sbuf_tensor("warena", [P, 32768], BF16))
        consts = kb.sb(es, [P, 3 * P], BF16, "consts")
        ident = consts.t[:, 0:P]
        tri = consts.t[:, P:2 * P]
        perm = consts.t[0:32, 2 * P:2 * P + 32]
        with ExitStack() as tmps:
            cstf = kb.sb(tmps, [P, 3 * P], F32, "cstf")
            dma("sp", cstf.t[:], cst[:, :], cstf, writes=[cstf])
            op("dve", lambda: V.tensor_copy(out=consts.t[:], in_=cstf.t[:]), reads=[cstf], writes=[consts])
            kb.barrier()
        NX = {}
        st4 = [kb.sb(es, [P, 4], F32, "st4") for _ in range(2)]
        nctr = [0]

        def load_grep(src):
            grep = NX["grep"]
            dma("sp", grep.t[:], src[:, :], grep, writes=[grep])

        def norm_load(src_ap):
            i = nctr[0]
            nctr[0] += 1
            xb, s4 = NX["xblk"][i % len(NX["xblk"])], st4[i % 2]
            dma("sp", xb.t[:], src_ap, xb, writes=[xb])
            return xb, s4

        def norm_block(src_ap, eps=1e-5, loaded=None):
            xn = NX["xn"]
            xb, s4 = loaded if loaded is not None else norm_load(src_ap)
            op("act", lambda: S.activation(out=xn.t[:], in_=xb.t[:], func=AF.Square, accum_out=s4.t[:, 0:1]),
               reads=[xb], writes=[xn, s4])
            op("dve", lambda: V.tensor_scalar(out=s4.t[:, 1:2], in0=s4.t[:, 0:1], scalar1=1.0 / D, scalar2=eps,
                                              op0=ALU.mult, op1=ALU.add), writes=[s4])
            op("act", lambda: S.activation(out=s4.t[:, 2:3], in_=s4.t[:, 1:2], func=AF.Sqrt), writes=[s4])
            op("dve", lambda: V.reciprocal(out=s4.t[:, 3:4], in_=s4.t[:, 2:3]), writes=[s4])
            return xb, s4

        def norm_to_T(src_ap, dstbuf, dst_of_dc4):
            norm_pre(src_ap)
            norm_post(dstbuf, dst_of_dc4)

        def norm_pre(src_ap, loaded=None):
            xb, s4 = norm_block(src_ap, loaded=loaded)
            xn = NX["xn"]
            op("act", lambda: S.activation(out=xn.t[:], in_=xb.t[:], func=AF.Copy, scale=s4.t[:, 3:4]),
               reads=[xb, s4], writes=[xn])

        def norm_post(dstbuf, dst_of_dc4):
            xn, gcol = NX["xn"], NX["gcol"]
            for d4 in range(DC // 4):
                tb = tp[d4 % 2]
                pe_group([(lambda k=k: T.transpose(out=tb.t[:, k * P:(k + 1) * P],
                                                   in_=xn.t[:, (d4 * 4 + k) * P:(d4 * 4 + k + 1) * P],
                                                   identity=ident)) for k in range(4)],
                         reads=[xn, consts], writes=[tb])
                dst = dst_of_dc4(d4 * 4)
                src = tb.t[:].rearrange("p (a b) -> p a b", b=P)
                gb_ = gcol.t[:, d4 * 4:d4 * 4 + 4].unsqueeze(2).broadcast_to([P, 4, P])
                op("dve", lambda: V.tensor_tensor(out=dst, in0=src, in1=gb_, op=ALU.mult), reads=[tb, gcol], accs=[dstbuf])

        def wload(wbuf, ap3, src3):
            dma("pool", ap3, src3, wbuf, writes=[wbuf])

        with ExitStack() as ph:
            T1 = c.T1
            nT = [kb.sb(ph, [P, DC, T1], BF16, "nT")] * 2
            NX["xblk"] = [kb.sb(ph, [P, D], F32, "xblk") for _ in range(2)]
            NX["xn"] = kb.sb(ph, [P, D], BF16, "xn")
            NX["gcol"] = kb.sb(ph, [P, DC], F32, "gcol")
            dma("sp", NX["gcol"].t[:], g1col[:, :], NX["gcol"], writes=[NX["gcol"]])
            w256a = [kb.view(warena[:, i * 8192:i * 8192 + DC * 256].rearrange("p (c f) -> p c f", f=256), "w256a")
                     for i in range(4)]
            ropc = kb.sb(ph, [32, 2], F32, "ropc")
            dma("sp", ropc.t[:], ropec[:, :], ropc, writes=[ropc])
            posi = kb.sb(ph, [32, T1], I32, "posi")
            ang = kb.sb(ph, [32, 2, T1], F32, "ang")
            angk = kb.sb(ph, [32, 2, T1], F32, "angk")
            angi = kb.sb(ph, [32, 2, T1], I32, "angi")
            tabs = [kb.sb(ph, [32, 2, T1], F32, "tab") for _ in range(2)]
            qk_sb = [kb.sb(ph, [P, T1], BF16, "qk_sb") for _ in range(3)]
            rt1 = [kb.sb(ph, [32, T1], F32, "rt1")] * 2
            rt2 = [kb.sb(ph, [32, T1], F32, "rt2")] * 2
            v_sb = [kb.sb(ph, [P, 512], BF16, "v_sb") for _ in range(4)]
            usb = [kb.sb(ph, [P, 16 + T1], F32, "usb") for _ in range(2)]
            sA = kb.sb(ph, [P, 16 + T1], F32, "sA")
            sB = kb.sb(ph, [P, 16 + T1], F32, "sB")
            sT = kb.sb(ph, [P, T1], F32, "sT")
            uhalo = kb.sb(ph, [P, 4 * PGC, 16], F32, "uhalo")
            invc = [kb.sb(ph, [P, T1], F32, "invc") for _ in range(2)]
            pooledT = [kb.sb(ph, [P, PGC, T1], BF16, "pooledT")] * 2
            posb = [kb.sb(ph, [P, T1], BF16, "posb") for _ in range(2)]
            wpool = [kb.sb(ph, [P, PGC, c.PG], BF16, "wpool") for _ in range(2)]
            psc = kb.sb(ph, [P, 4 * PGC], F32, "psc")
            dma("sp", psc.t[:], pscol[:, :], psc, writes=[psc])
            op("pool", lambda: G.memset(uhalo.t[:], 0.0), writes=[uhalo])
            accb = banks[0:3]
            swb = banks[3]
            plb = banks[4]
            acc_i = [0]
            qk_i = [0]

            def next_acc():
                b = accb[acc_i[0] % 3]
                acc_i[0] += 1
                return b

            TWO_PI = 2.0 * math.pi
            C1 = 6.28125
            C2 = TWO_PI - C1

            def rope_tables(t):
                tb = tabs[t % 2]
                s0 = t * T1
                dma("sp", posi.t[:], posr[:, s0:s0 + T1], posi, writes=[posi])
                op("dve", lambda: V.tensor_copy(out=ang.t[:, 0], in_=posi.t[:]), reads=[posi], writes=[ang])
                op("dve", lambda: V.tensor_scalar(out=ang.t[:, 0], in0=ang.t[:, 0], scalar1=ropc.t[:, 0:1],
                                                  scalar2=None, op0=ALU.mult), reads=[ropc], writes=[ang])
                op("dve", lambda: V.tensor_scalar(out=ang.t[:, 1], in0=ang.t[:, 0], scalar1=math.pi / 2,
                                                  scalar2=None, op0=ALU.add), writes=[ang])
                op("dve", lambda: V.tensor_scalar(out=angk.t[:], in0=ang.t[:], scalar1=1.0 / TWO_PI, scalar2=0.5,
                                                  op0=ALU.mult, op1=ALU.add), reads=[ang], writes=[angk])
                op("dve", lambda: V.tensor_copy(out=angi.t[:], in_=angk.t[:]), reads=[angk], writes=[angi])
                op("dve", lambda: V.tensor_copy(out=angk.t[:], in_=angi.t[:]), reads=[angi], writes=[angk])
                op("dve", lambda: V.scalar_tensor_tensor(out=ang.t[:], in0=angk.t[:], scalar=-C1, in1=ang.t[:],
                                                         op0=ALU.mult, op1=ALU.add), reads=[angk], writes=[ang])
                op("dve", lambda: V.scalar_tensor_tensor(out=ang.t[:], in0=angk.t[:], scalar=-C2, in1=ang.t[:],
                                                         op0=ALU.mult, op1=ALU.add), reads=[angk], writes=[ang])
                op("dve", lambda: V.tensor_scalar(out=angk.t[:], in0=ang.t[:], scalar1=math.pi, scalar2=-TWO_PI,
                                                  op0=ALU.is_gt, op1=ALU.mult), reads=[ang], writes=[angk])
                op("dve", lambda: V.tensor_tensor(out=ang.t[:], in0=ang.t[:], in1=angk.t[:], op=ALU.add),
                   reads=[angk], writes=[ang])
                op("dve", lambda: V.tensor_scalar(out=angk.t[:], in0=ang.t[:], scalar1=-math.pi, scalar2=TWO_PI,
                                                  op0=ALU.is_lt, op1=ALU.mult), reads=[ang], writes=[angk])
                op("dve", lambda: V.tensor_tensor(out=ang.t[:], in0=ang.t[:], in1=angk.t[:], op=ALU.add),
                   reads=[angk], writes=[ang])
                op("dve", lambda: V.tensor_scalar(out=ang.t[:], in0=ang.t[:], scalar1=math.pi, scalar2=-math.pi,
                                                  op0=ALU.min, op1=ALU.max), writes=[ang])
                op("act", lambda: S.activation(out=tb.t[:], in_=ang.t[:], func=AF.Sin), reads=[ang], writes=[tb])
                op("dve", lambda: V.tensor_scalar(out=tb.t[:, 0], in0=tb.t[:, 0], scalar1=ropc.t[:, 1:2],
                                                  scalar2=None, op0=ALU.mult), reads=[ropc], writes=[tb])

            pre_state = {}

            def xrows(t, bi):
                s0 = t * T1
                return xs[s0 + bi * P:s0 + (bi + 1) * P, :]

            def prep_early(t):
                l0 = norm_load(xrows(t, 0))
                l1 = norm_load(xrows(t, 1))
                norm_pre(None, loaded=l0)
                rope_tables(t)
                pre_state[t] = l1

            def prep_tile(t):
                dst = nT[t % 2]
                nb = T1 // P
                early = t in pre_state
                loads = {}
                if early:
                    loads[1] = pre_state[t]
                for bi in range(nb):
                    if not (early and bi == 0):
                        norm_pre(xrows(t, bi), loaded=loads.get(bi))
                    norm_post(dst, lambda dc0, bi=bi: dst.t[:, dc0:dc0 + 4, bi * P:(bi + 1) * P])
                    if early and bi + 2 < nb:
                        loads[bi + 2] = norm_load(xrows(t, bi + 2))
                if not early:
                    rope_tables(t)

            wl = []
            for t in range(c.NT1):
                kinds = ["k", "v"] + (["q", "u"] if t >= c.OWN_T0 else [])
                for kind in kinds:
                    for j in range(AW // 256):
                        wl.append((t, kind, j))
            col0 = {"q": 0, "k": AW, "v": 2 * AW, "u": 3 * AW}

            def issue_w(i):
                t, kind, j = wl[i]
                f0 = col0[kind] + j * 256
                wload(w256a[i % 4], w256a[i % 4].t[:], w_in_v[:, :, f0:f0 + 256])

            pend_rope = []

            def flush_rope():
                while pend_rope:
                    pend_rope.pop(0)()

            def qk_chunk(t, kind, ch, wb, sub):
                s0 = t * T1
                acc = next_acc()
                n = nT[t % 2]
                pe_group([(lambda dc=dc: T.matmul(acc.t[:], lhsT=wb.t[:, dc, sub * P:(sub + 1) * P], rhs=n.t[:, dc, :],
                                                  start=(dc == 0), stop=(dc == DC - 1))) for dc in range(DC)],
                         reads=[wb, n], writes=[acc])
                qs = qk_sb[qk_i[0] % 3]
                qk_i[0] += 1
                scale = (128.0 ** -0.5) if kind == "q" else 1.0
                op("act", lambda: S.activation(out=qs.t[:], in_=acc.t[:], func=AF.Copy, scale=scale),
                   reads=[acc], writes=[qs])
                flush_rope()
                tb = tabs[t % 2]
                r1, r2 = rt1[ch % 2], rt2[ch % 2]
                dstT = qT if kind == "q" else kT

                def rope():
                    pe_group([lambda: T.matmul(swb.t[0:32, :], lhsT=perm, rhs=qs.t[0:32, :], start=True, stop=True)],
                             reads=[qs, consts], writes=[swb])
                    op("dve", lambda: V.tensor_tensor(out=r1.t[:], in0=swb.t[0:32, :], in1=tb.t[:, 0], op=ALU.mult),
                       reads=[swb, tb], writes=[r1])
                    op("dve", lambda: V.tensor_tensor(out=r2.t[:], in0=qs.t[0:32, :], in1=tb.t[:, 1], op=ALU.mult),
                       reads=[qs, tb], writes=[r2])
                    op("dve", lambda: V.tensor_tensor(out=qs.t[0:32, :], in0=r1.t[:], in1=r2.t[:], op=ALU.add),
                       reads=[r1, r2], writes=[qs])
                    dma("sp", dstT[ch, :, s0:s0 + T1], qs.t[:], qs, reads=[qs])
                pend_rope.append(rope)

            def v_tile(t, j, wb):
                s0 = t * T1
                n = nT[t % 2]
                for bi in range(T1 // P):
                    acc = next_acc()
                    pe_group([(lambda dc=dc: T.matmul(acc.t[:, 0:256], lhsT=n.t[:, dc, bi * P:(bi + 1) * P], rhs=wb.t[:, dc, :],
                                                      start=(dc == 0), stop=(dc == DC - 1))) for dc in range(DC)],
                             reads=[wb, n], writes=[acc])
                    vs = v_sb[bi]
                    if j % 2 == 0:
                        op("act", lambda: S.copy(out=vs.t[:, 0:256], in_=acc.t[:, 0:256]), reads=[acc], writes=[vs])
                    else:
                        op("act", lambda: S.copy(out=vs.t[:, 256:512], in_=acc.t[:, 0:256]), reads=[acc], accs=[vs])
                        j2 = j // 2
                        dma("sp", vv[s0 + bi * P:s0 + (bi + 1) * P, j2 * 512:(j2 + 1) * 512], vs.t[:], vs, reads=[vs])

            def u_chunk(t, cu, wb, sub):
                s0 = t * T1
                g = cu // PGC
                cc = cu % PGC
                acc = next_acc()
                n = nT[t % 2]
                wpl, ivc = wpool[g % 2], invc[g % 2]
                if cc == 0:
                    dma("pool", wpl.t[:], w_pool[g].rearrange("(c p) d -> p c d", p=P), wpl, writes=[wpl])
                    o0 = t * T1 - c.OWN_S0
                    dma("sp", ivc.t[:], invcnt[:, g, o0:o0 + T1], ivc, writes=[ivc])
                pe_group([(lambda dc=dc: T.matmul(acc.t[:], lhsT=wb.t[:, dc, sub * P:(sub + 1) * P], rhs=n.t[:, dc, :],
                                                  start=(dc == 0), stop=(dc == DC - 1))) for dc in range(DC)],
                         reads=[wb, n], writes=[acc])
                ub = usb[cu % 2]
                op("act", lambda: S.copy(out=ub.t[:, 16:16 + T1], in_=acc.t[:]), reads=[acc], writes=[ub])
                op("pool", lambda: G.tensor_copy(out=ub.t[:, 0:16], in_=uhalo.t[:, cu, :]), reads=[uhalo], accs=[ub])
                op("pool", lambda: G.tensor_copy(out=uhalo.t[:, cu, :], in_=ub.t[:, T1:T1 + 16]), reads=[ub], writes=[uhalo])
                W = 16 + T1
                cur = ub
                for step, (sh, dst) in enumerate([(1, sA), (2, sB), (4, sA), (8, sB)][:g + 1]):
                    lo = 2 * sh - 1
                    src = cur
                    op("pool", lambda src=src, dst=dst, sh=sh, lo=lo: G.tensor_tensor(
                        out=dst.t[:, lo:W], in0=src.t[:, lo:W], in1=src.t[:, lo - sh:W - sh], op=ALU.add),
                       reads=[src], writes=[dst])
                    cur = dst
                pb = pooledT[g % 2]
                op("dve", lambda: V.tensor_tensor(out=sT.t[:], in0=cur.t[:, 16:W], in1=ivc.t[:], op=ALU.mult),
                   reads=[cur, ivc], writes=[sT])
                wr = dict(writes=[pb]) if cc == 0 else dict(accs=[pb])
                op("dve", lambda: V.tensor_tensor(out=pb.t[:, cc, :], in0=sT.t[:], in1=ub.t[:, 16:W], op=ALU.subtract),
                   reads=[sT, ub], **wr)
                if cc == PGC - 1:
                    for do in range(PGC):
                        pe_group([(lambda k=k: T.matmul(plb.t[:], lhsT=wpl.t[:, k, do * P:(do + 1) * P],
                                                        rhs=pb.t[:, k, :], start=(k == 0), stop=(k == PGC - 1)))
                                  for k in range(PGC)], reads=[wpl, pb], writes=[plb])
                        po = posb[do % 2]
                        col = g * PGC + do
                        op("act", lambda: S.activation(out=po.t[:], in_=plb.t[:], func=AF.Copy,
                                                       scale=psc.t[:, col:col + 1]), reads=[plb, psc], writes=[po])
                        dma("sp", mixT[NHC + col, :, s0:s0 + T1], po.t[:], po, reads=[po])

            prep_tile(0)
            for i0 in range(min(3, len(wl))):
                issue_w(i0)
            first_of_tile = {}
            for i, (t, kind, j) in enumerate(wl):
                first_of_tile.setdefault(t, i)
            for i, (t, kind, j) in enumerate(wl):
                if i + 3 < len(wl):
                    issue_w(i + 3)
                if i == first_of_tile[t] and t > 0:
                    flush_rope()
                    prep_tile(t)
                wb = w256a[i % 4]
                if kind in ("q", "k"):
                    for sub in range(2):
                        qk_chunk(t, kind, j * 2 + sub, wb, sub)
                elif kind == "v":
                    v_tile(t, j, wb)
                else:
                    for sub in range(2):
                        u_chunk(t, j * 2 + sub, wb, sub)
            flush_rope()
            kb.barrier()

        with ExitStack() as ph:
            NQB, QB0 = c.NQB, c.QB0
            Kt = [[kb.sb(ph, [P, SEQ], BF16, "Kt") for _ in range(2)] for _ in range(2)]
            Qt = [[kb.sb(ph, [P, NQB * P], BF16, "Qt") for _ in range(2)] for _ in range(2)]
            Vx = [kb.sb(ph, [P, NB, 258], BF16, "Vx") for _ in range(2)]
            kvf = kb.sb(ph, [P, NB], F32, "kvf")
            dma("sp", kvf.t[:], kvalid[:, :], kvf, writes=[kvf])
            pT = [kb.sb(ph, [P, 512], BF16, "pT") for _ in range(4)]
            o0b = kb.sb(ph, [P, 4, 256], F32, "o0b")
            ob = kb.sb(ph, [P, 4, 256], F32, "ob")
            onb = kb.sb(ph, [P, 4, 256], BF16, "onb")
            junk = kb.sb(ph, [P, 256], BF16, "junk")
            sm = kb.sb(ph, [P, 8, 4], F32, "sm")
            slc = kb.sb(ph, [P, 2], F32, "slc")
            aT = [kb.sb(ph, [P, 2, 512], BF16, "aT") for _ in range(2)]
            lamb = kb.sb(ph, [P, 4, P], F32, "lamb")
            lw = kb.sb(ph, [P, 8], F32, "lw")
            dma("sp", lamb.t[:], lamrep[:, :, :], lamb, writes=[lamb])
            dma("sp", slc.t[:], slncol[:, :], slc, writes=[slc])
            lprod = kb.sb(ph, [P, 2, P], F32, "lprod")
            op("dve", lambda: V.tensor_tensor(out=lprod.t[:, 0], in0=lamb.t[:, 0], in1=lamb.t[:, 1], op=ALU.mult),
               reads=[lamb], writes=[lprod])
            op("dve", lambda: V.tensor_tensor(out=lprod.t[:, 1], in0=lamb.t[:, 2], in1=lamb.t[:, 3], op=ALU.mult),
               reads=[lamb], writes=[lprod])
            op("dve", lambda: V.tensor_reduce(out=lw.t[:, 0:2], in_=lprod.t[:], axis=AX.X, op=ALU.add),
               reads=[lprod], writes=[lw])
            op("act", lambda: S.activation(out=lw.t[:, 2:4], in_=lw.t[:, 0:2], func=AF.Exp), writes=[lw])
            lambda_init = 0.8 - 0.6 * math.exp(-0.3 * 0)
            op("dve", lambda: V.tensor_tensor(out=lw.t[:, 4:5], in0=lw.t[:, 3:4], in1=lw.t[:, 2:3], op=ALU.subtract),
               writes=[lw])
            op("dve", lambda: V.tensor_scalar(out=lw.t[:, 4:5], in0=lw.t[:, 4:5], scalar1=-lambda_init, scalar2=None,
                                              op0=ALU.add), writes=[lw])
            op("dve", lambda: V.tensor_scalar(out=slc.t[:], in0=slc.t[:], scalar1=1.0 - lambda_init, scalar2=None,
                                              op0=ALU.mult), writes=[slc])
            sbk = banks[0:2]
            acb = banks[2:6]
            groups = [[QB0]] + [list(range(q, min(q + 4, NB))) for q in range(QB0 + 1, NB, 4)]

            def load_head(h):
                hb = h % 2
                for cc in range(2):
                    dma("sp", Kt[hb][cc].t[:], kT[h * 2 + cc, :, :], Kt[hb][cc], writes=[Kt[hb][cc]])
                    dma("sp", Qt[hb][cc].t[:], qT[h * 2 + cc, :, c.H0:SEQ], Qt[hb][cc], writes=[Qt[hb][cc]])
                dma("sp", Vx[hb].t[:, :, 0:256], vv.rearrange("(b p) e -> p b e", p=P)[:, :, h * 256:(h + 1) * 256],
                    Vx[hb], writes=[Vx[hb]])
                op("pool", lambda: G.tensor_copy(out=Vx[hb].t[:, :, 256:257], in_=kvf.t[:].unsqueeze(2)),
                   reads=[kvf], accs=[Vx[hb]])

            sidx = [0]
            eidx = [0]
            load_head(0)
            for h in range(NH):
                hb = h % 2
                if h + 1 < NH:
                    load_head(h + 1)
                for grp in groups:
                    g0, gl = grp[0], grp[-1]
                    for cc in range(2):
                        K, Q, Vb = Kt[hb][cc], Qt[hb][cc], Vx[hb]
                        pend = []

                        def s_step(kbk):
                            fv = max(g0, kbk)
                            n = (gl - fv + 1) * P
                            i = sidx[0]
                            sidx[0] += 1
                            sb_, pb_ = sbk[i % 2], pT[i % 4]
                            pe_group([lambda: T.matmul(sb_.t[:, 0:n], lhsT=K.t[:, kbk * P:(kbk + 1) * P],
                                                       rhs=Q.t[:, (fv - QB0) * P:(gl + 1 - QB0) * P], start=True, stop=True)],
                                     reads=[K, Q], writes=[sb_])
                            op("act", lambda: S.activation(out=pb_.t[:, 0:n], in_=sb_.t[:, 0:n], func=AF.Exp),
                               reads=[sb_], writes=[pb_])
                            if kbk >= g0:
                                op("dve", lambda: V.tensor_tensor(out=pb_.t[:, 0:P], in0=pb_.t[:, 0:P], in1=tri, op=ALU.mult),
                                   reads=[consts], writes=[pb_])
                            return (kbk, fv, pb_)

                        def pv_step(item):
                            kbk, fv, pb_ = item
                            fns = [(lambda qb=qb: T.matmul(acb[qb - g0].t[:, 0:257],
                                                           lhsT=pb_.t[:, (qb - fv) * P:(qb - fv + 1) * P],
                                                           rhs=Vb.t[:, kbk, 0:257], start=(kbk == 0), stop=(kbk == qb)))
                                   for qb in range(fv, gl + 1)]
                            accl = [acb[qb - g0] for qb in range(fv, gl + 1)]
                            if kbk == 0:
                                pe_group(fns, reads=[pb_, Vb], writes=accl)
                            else:
                                pe_group(fns, reads=[pb_, Vb], accs=accl)

                        for kbk in range(gl + 1):
                            pend.append(s_step(kbk))
                            if len(pend) > 2:
                                pv_step(pend.pop(0))
                        while pend:
                            pv_step(pend.pop(0))
                        nq = len(grp)
                        accs_ = acb[0:nq]
                        a4 = bankT[:, 2:2 + nq, :]
                        st = sm.t
                        op("dve", lambda: V.tensor_scalar(out=st[:, 0, 0:nq].unsqueeze(2), in0=a4[:, :, 256:257], scalar1=1e-20,
                                                          scalar2=None, op0=ALU.max), reads=accs_, writes=[sm])
                        op("dve", lambda: V.reciprocal(out=st[:, 1, 0:nq], in_=st[:, 0, 0:nq]), writes=[sm])
                        if cc == 0:
                            op("dve", lambda: V.tensor_tensor(out=o0b.t[:, 0:nq, :], in0=a4[:, :, 0:256],
                                                              in1=st[:, 1, 0:nq].unsqueeze(2).broadcast_to([P, nq, 256]),
                                                              op=ALU.mult), reads=accs_ + [sm], writes=[o0b])
                        else:
                            op("dve", lambda: V.tensor_scalar(out=st[:, 2, 0:nq], in0=st[:, 1, 0:nq], scalar1=lw.t[:, 4:5],
                                                              scalar2=None, op0=ALU.mult), reads=[lw], writes=[sm])
                            op("dve", lambda: V.tensor_tensor(out=ob.t[:, 0:nq, :], in0=a4[:, :, 0:256],
                                                              in1=st[:, 2, 0:nq].unsqueeze(2).broadcast_to([P, nq, 256]),
                                                              op=ALU.mult), reads=accs_ + [sm], writes=[ob])
                            op("pool", lambda: G.tensor_tensor(out=ob.t[:, 0:nq, :], in0=ob.t[:, 0:nq, :], in1=o0b.t[:, 0:nq, :],
                                                               op=ALU.add), reads=[o0b], writes=[ob])
                            for qi in range(nq):
                                op("act", lambda: S.activation(out=junk.t[:], in_=ob.t[:, qi, :], func=AF.Square,
                                                               accum_out=st[:, 3, qi:qi + 1]), reads=[ob], writes=[junk], accs=[sm])
                            op("dve", lambda: V.tensor_scalar(out=st[:, 4, 0:nq], in0=st[:, 3, 0:nq], scalar1=1.0 / 256,
                                                              scalar2=1e-5, op0=ALU.mult, op1=ALU.add), writes=[sm])
                            op("act", lambda: S.activation(out=st[:, 5, 0:nq], in_=st[:, 4, 0:nq], func=AF.Sqrt), writes=[sm])
                            op("dve", lambda: V.reciprocal(out=st[:, 6, 0:nq], in_=st[:, 5, 0:nq]), writes=[sm])
                            op("dve", lambda: V.tensor_tensor(out=onb.t[:, 0:nq, :], in0=ob.t[:, 0:nq, :],
                                                              in1=st[:, 6, 0:nq].unsqueeze(2).broadcast_to([P, nq, 256]),
                                                              op=ALU.mult), reads=[ob, sm], writes=[onb])
                            for ec in range(2):
                                pe_group([(lambda qi=qi: T.transpose(out=tp[ec].t[:, qi * P:(qi + 1) * P],
                                                                     in_=onb.t[:, qi, ec * P:(ec + 1) * P], identity=ident))
                                          for qi in range(nq)], reads=[onb, consts], writes=[tp[ec]])
                            ng = nq * P
                            ab = aT[eidx[0] % 2]
                            eidx[0] += 1
                            op("dve", lambda: V.tensor_scalar(out=ab.t[:, 0, 0:ng], in0=tp[0].t[:, 0:ng], scalar1=slc.t[:, 0:1],
                                                              scalar2=None, op0=ALU.mult), reads=[tp[0], slc], writes=[ab])
                            op("act", lambda: S.activation(out=ab.t[:, 1, 0:ng], in_=tp[1].t[:, 0:ng], func=AF.Copy,
                                                           scale=slc.t[:, 1:2]), reads=[tp[1], slc], accs=[ab])
                            dma("sp", mixT_v[:, h * 2:h * 2 + 2, g0 * P:g0 * P + ng], ab.t[:, :, 0:ng], ab, reads=[ab])
            kb.barrier()

        with ExitStack() as ph:
            n2T = kb.sb(ph, [P, DC, 512], BF16, "n2T")
            n2h = kb.sb(ph, [P, DC, 2], BF16, "n2h")
            big = ph.enter_context(nc.sbuf_tensor("big", [P, max(FC * 512, 10 * D, DC * 640)], BF16))
            NX["grep"] = kb.view(big[:, 2 * D:4 * D].bitcast(F32), "grep3")
            NX["xblk"] = [kb.view(big[:, (4 + 2 * i) * D:(6 + 2 * i) * D].bitcast(F32), "xblk3") for i in range(2)]
            NX["xn"] = kb.view(big[:, 8 * D:9 * D], "xn3")
            NX["gcol"] = kb.sb(ph, [P, DC], F32, "gcol3")
            dma("sp", NX["gcol"].t[:], g2col[:, :], NX["gcol"], writes=[NX["gcol"]])
            actT = kb.view(big[:, 0:FC * 512].rearrange("p (c t) -> p c t", t=512), "actT")
            gcarry = kb.sb(ph, [P, FC, 2], F32, "gcarry")
            cwc = kb.sb(ph, [P, FC, 3], F32, "cwc")
            cbc = kb.sb(ph, [P, FC], F32, "cbc")
            dma("sp", cwc.t[:], cwcol[:, :, :], cwc, writes=[cwc])
            dma("sp", cbc.t[:], cbcol[:, :], cbc, writes=[cbc])
            sl_in4 = [kb.sb(ph, [P, 512], F32, "sl_in") for _ in range(4)]
            sl_in = sl_in4[0:2]
            sl_out = [kb.sb(ph, [P, 512], F32, "sl_out") for _ in range(2)]
            gs = [kb.sb(ph, [P, 2 + 512], F32, "gs") for _ in range(2)]
            gh = kb.sb(ph, [2, 256], BF16, "gh")
            cv = [kb.sb(ph, [P, 512], F32, "cv") for _ in range(2)]
            sil = [kb.sb(ph, [P, 512], BF16, "sil") for _ in range(2)]
            sli = [0]

            mt = kb.view(big[:, 0:DC * 640].rearrange("p (c t) -> p c t", t=640), "mt")
            w512o = [kb.view(warena[:, i * 16384:i * 16384 + DC * 512].rearrange("p (c f) -> p c f", f=512), "w512o")
                     for i in range(2)]
            w256 = [kb.view(warena[:, i * 8192:i * 8192 + DC * 256].rearrange("p (c f) -> p c f", f=256), "w256")
                    for i in range(4)]
            w512d = [kb.view(warena[:, i * 16384:(i + 1) * 16384].rearrange("p (c f) -> p c f", f=512), "w512d")
                     for i in range(2)]
            NXe = {"grep": kb.view(big[:, 4 * D:6 * D].bitcast(F32), "grepe"),
                   "xblk": [kb.view(big[:, (6 + 2 * i) * D:(8 + 2 * i) * D].bitcast(F32), "xblke") for i in range(2)],
                   "xn": kb.view(n2T.t[:, 0:D // 512, :].rearrange("p a b -> p (a b)"), "xne")}
            ybs = [kb.view(big[:, 2 * i * D:(2 * i + 2) * D].bitcast(F32), "yb") for i in range(2)]
            a_pref = [False]
            for ti in range(c.NMAIN):
                m0 = c.HALF + ti * 512
                blocks = ([c.H0] if ti == 0 else []) + [m0 + k * P for k in range(4)]
                TW = len(blocks) * P
                s0 = blocks[0]
                dma("sp", mt.t[:, :, 0:TW], mixT_v[:, :, s0:s0 + TW], mt, writes=[mt])
                w512 = w512o
                nsl = D // 512
                if not a_pref[0]:
                    wload(w512[0], w512[0].t[:], w_out_v[:, :, 0:512])
                ai = 0
                for s in range(nsl):
                    if s + 1 < nsl and not (a_pref[0] and s == 0):
                        wload(w512[(s + 1) % 2], w512[(s + 1) % 2].t[:], w_out_v[:, :, (s + 1) * 512:(s + 2) * 512])
                    wb = w512[s % 2]
                    for bi, bs in enumerate(blocks):
                        acc = banks[ai % 3]
                        ai += 1
                        k = sli[0] % 2
                        sli[0] += 1
                        xi, ho = sl_in[k], sl_out[k]
                        dma("sp", xi.t[:], xs[bs:bs + P, s * 512:(s + 1) * 512], xi, writes=[xi])
                        pe_group([(lambda fc=fc: T.matmul(acc.t[:], lhsT=mt.t[:, fc, bi * P:(bi + 1) * P], rhs=wb.t[:, fc, :],
                                                          start=(fc == 0), stop=(fc == DC - 1))) for fc in range(DC)],
                                 reads=[mt, wb], writes=[acc])
                        op("dve", lambda: V.tensor_tensor(out=ho.t[:], in0=acc.t[:], in1=xi.t[:], op=ALU.add),
                           reads=[acc, xi], writes=[ho])
                        r0 = bs - c.H0
                        dma("sp", hbuf[r0:r0 + P, s * 512:(s + 1) * 512], ho.t[:], ho, reads=[ho])
                kb.barrier()
                wgb, wub = w256[0:2], w256[2:4]
                nwt = (FC + 1) // 2

                def issue_gu(i):
                    f0 = i * 256
                    fw = min(256, c.DFF - f0)
                    wload(wgb[i % 2], wgb[i % 2].t[:, :, 0:fw], w_gate_v[:, :, f0:f0 + fw])
                    wload(wub[i % 2], wub[i % 2].t[:, :, 0:fw], w_up_v[:, :, f0:f0 + fw])
                issue_gu(0)
                if nwt > 1:
                    issue_gu(1)
                for bi, bs in enumerate(blocks):
                    r0 = bs - c.H0
                    if ti == 0 and bi == 0:
                        xb, s4 = norm_block(hbuf[r0:r0 + P, :])
                        xn, gcol = NX["xn"], NX["gcol"]
                        op("act", lambda: S.activation(out=xn.t[:], in_=xb.t[:], func=AF.Copy, scale=s4.t[:, 3:4]),
                           reads=[xb, s4], writes=[xn])
                        for d4 in range(DC // 4):
                            tb = tp[d4 % 2]
                            pe_group([(lambda k=k: T.transpose(out=tb.t[:, k * P:(k + 1) * P],
                                                               in_=xn.t[:, (d4 * 4 + k) * P:(d4 * 4 + k + 1) * P],
                                                               identity=ident)) for k in range(4)],
                                     reads=[xn, consts], writes=[tb])
                            op("dve", lambda: V.tensor_tensor(
                                out=n2h.t[:, d4 * 4:d4 * 4 + 4, :],
                                in0=tb.t[:].rearrange("p (a b) -> p a b", b=P)[:, :, P - 2:P],
                                in1=gcol.t[:, d4 * 4:d4 * 4 + 4].unsqueeze(2).broadcast_to([P, 4, 2]), op=ALU.mult),
                               reads=[tb, gcol], accs=[n2h])
                    else:
                        mb = bi - (1 if ti == 0 else 0)
                        norm_to_T(hbuf[r0:r0 + P, :], n2T,
                                  lambda dc0, mb=mb: n2T.t[:, dc0:dc0 + 4, mb * P:(mb + 1) * P])
                kb.barrier()
                for i in range(nwt):
                    if i == 0 and nwt > 1:
                        pass
                    elif i + 1 < nwt:
                        issue_gu(i + 1)
                    wg_, wu_ = wgb[i % 2], wub[i % 2]
                    nsub = min(2, FC - 2 * i)
                    if ti == 0:
                        hb_, fw_ = banks[4], nsub * P
                        pe_group([(lambda dc=dc: T.matmul(hb_.t[0:2, 0:fw_], lhsT=n2h.t[:, dc, :], rhs=wg_.t[:, dc, 0:fw_],
                                                          start=(dc == 0), stop=(dc == DC - 1)))
                                  for dc in range(DC)], reads=[wg_, n2h], writes=[hb_])
                        op("act", lambda: S.copy(out=gh.t[0:2, 0:fw_], in_=hb_.t[0:2, 0:fw_]), reads=[hb_], writes=[gh])
                    for sub in range(nsub):
                        j = 2 * i + sub
                        gb, ub_ = banks[j % 2], banks[2 + j % 2]
                        g_, cv_, sl_ = gs[j % 2], cv[j % 2], sil[j % 2]
                        if ti == 0:
                            if sub == 0:
                                pe_group([(lambda k=k: T.transpose(out=tp[0].t[:, 2 * k:2 * k + 2],
                                                                   in_=gh.t[0:2, k * P:(k + 1) * P],
                                                                   identity=consts.t[0:2, 0:2])) for k in range(nsub)],
                                         reads=[gh, consts], writes=[tp[0]])
                            op("act", lambda: S.copy(out=g_.t[:, 0:2], in_=tp[0].t[:, 2 * sub:2 * sub + 2]),
                               reads=[tp[0]], writes=[g_])
                        else:
                            op("act", lambda: S.copy(out=g_.t[:, 0:2], in_=gcarry.t[:, j, :]), reads=[gcarry], writes=[g_])
                        pe_group([(lambda dc=dc: T.matmul(gb.t[:], lhsT=wg_.t[:, dc, sub * P:(sub + 1) * P], rhs=n2T.t[:, dc, :],
                                                          start=(dc == 0), stop=(dc == DC - 1))) for dc in range(DC)],
                                 reads=[wg_, n2T], writes=[gb])
                        pe_group([(lambda dc=dc: T.matmul(ub_.t[:], lhsT=wu_.t[:, dc, sub * P:(sub + 1) * P], rhs=n2T.t[:, dc, :],
                                                          start=(dc == 0), stop=(dc == DC - 1))) for dc in range(DC)],
                                 reads=[wu_, n2T], writes=[ub_])
                        op("act", lambda: S.copy(out=g_.t[:, 2:514], in_=gb.t[:]), reads=[gb], accs=[g_])
                        op("pool", lambda: G.tensor_copy(out=gcarry.t[:, j, :], in_=g_.t[:, 512:514]), reads=[g_], accs=[gcarry])
                        op("dve", lambda: V.tensor_scalar(out=cv_.t[:], in0=g_.t[:, 2:514], scalar1=cwc.t[:, j, 2:3],
                                                          scalar2=cbc.t[:, j:j + 1], op0=ALU.mult, op1=ALU.add),
                           reads=[g_, cwc, cbc], writes=[cv_])
                        op("dve", lambda: V.scalar_tensor_tensor(out=cv_.t[:], in0=g_.t[:, 1:513], scalar=cwc.t[:, j, 1:2],
                                                                 in1=cv_.t[:], op0=ALU.mult, op1=ALU.add),
                           reads=[g_], writes=[cv_])
                        op("dve", lambda: V.scalar_tensor_tensor(out=cv_.t[:], in0=g_.t[:, 0:512], scalar=cwc.t[:, j, 0:1],
                                                                 in1=cv_.t[:], op0=ALU.mult, op1=ALU.add),
                           reads=[g_], writes=[cv_])
                        op("act", lambda: S.activation(out=sl_.t[:], in_=cv_.t[:], func=AF.Silu), reads=[cv_], writes=[sl_])
                        op("dve", lambda: V.tensor_tensor(out=actT.t[:, j, :], in0=sl_.t[:], in1=ub_.t[:], op=ALU.mult),
                           reads=[sl_, ub_], accs=[actT])
                kb.barrier()
                w512 = w512d
                fgs = [(f0, min(32, FC - f0)) for f0 in range(0, FC, 32)]
                dl = [(s, f0, nf) for s in range(nsl) for (f0, nf) in fgs]

                def issue_d(i):
                    s, f0, nf = dl[i]
                    wload(w512[i % 2], w512[i % 2].t[:, 0:nf, :], w_down_v[:, f0:f0 + nf, s * 512:(s + 1) * 512])
                issue_d(0)
                for i, (s, f0, nf) in enumerate(dl):
                    if i + 1 < len(dl):
                        issue_d(i + 1)
                    wb = w512[i % 2]
                    aset = banks[(s % 2) * 4:(s % 2) * 4 + 4]
                    if f0 == 0:
                        for bi in range(4):
                            r0 = m0 + bi * P - c.H0
                            dma("sp", sl_in4[bi].t[:], hbuf[r0:r0 + P, s * 512:(s + 1) * 512], sl_in4[bi], writes=[sl_in4[bi]])
                    for a in range(nf):
                        j = f0 + a
                        fns = [(lambda bi=bi: T.matmul(aset[bi].t[:], lhsT=actT.t[:, j, bi * P:(bi + 1) * P], rhs=wb.t[:, a, :],
                                                       start=(j == 0), stop=(j == FC - 1))) for bi in range(4)]
                        if j == 0:
                            pe_group(fns, reads=[actT, wb], writes=list(aset))
                        else:
                            pe_group(fns, reads=[actT, wb], accs=list(aset))
                    if f0 + nf == FC:
                        for bi in range(4):
                            acc = aset[bi]
                            k = sli[0] % 2
                            sli[0] += 1
                            xi, ho = sl_in4[bi], sl_out[k]
                            r0 = m0 + bi * P - c.H0
                            op("dve", lambda: V.tensor_tensor(out=ho.t[:], in0=acc.t[:], in1=xi.t[:], op=ALU.add),
                               reads=[acc, xi], writes=[ho])
                            dma("sp", hbuf[r0:r0 + P, s * 512:(s + 1) * 512], ho.t[:], ho, reads=[ho])
                kb.barrier()
                NXb = dict(NX)
                NX.update(NXe)
                load_grep(gfrep)
                a_pref[0] = False
                if ti + 1 < c.NMAIN:
                    for k_ in range(2):
                        wload(w512o[k_], w512o[k_].t[:], w_out_v[:, :, k_ * 512:(k_ + 1) * 512])
                    a_pref[0] = True
                grep = NX["grep"]
                for bi in range(4):
                    yb = ybs[bi % 2]
                    r0 = m0 + bi * P - c.H0
                    xb, s4 = norm_block(hbuf[r0:r0 + P, :])
                    op("dve", lambda: V.scalar_tensor_tensor(out=yb.t[:], in0=xb.t[:], scalar=s4.t[:, 3:4], in1=grep.t[:],
                                                             op0=ALU.mult, op1=ALU.mult),
                       reads=[xb, s4, grep], writes=[yb])
                    o_r = m0 + bi * P - c.HALF
                    dma("sp", y[o_r:o_r + P, :], yb.t[:], yb, reads=[yb])
                kb.barrier()
                NX.clear()
                NX.update(NXb)
    return nc


def host_inputs(cfg, inputs):
    c = cfg
    f32 = np.float32
    x = np.asarray(inputs["x"], f32)
    positions = np.asarray(inputs["positions"]).astype(np.int32)
    l = 0
    inv_freq = (1.0 / (500000.0 ** (np.arange(0, 32, 2, dtype=f32) / f32(32)))).astype(f32)
    ropec = np.zeros((32, 2), f32)
    ropec[:, 0] = np.concatenate([inv_freq, inv_freq])
    ropec[:16, 1] = -1.0
    ropec[16:, 1] = 1.0
    cst = np.zeros((P, 3 * P), f32)
    cst[:, 0:P] = np.eye(P, dtype=f32)
    kk, qq = np.meshgrid(np.arange(P), np.arange(P), indexing="ij")
    cst[:, P:2 * P] = (qq >= kk).astype(f32)
    for m in range(32):
        cst[(m + 16) % 32, 2 * P + m] = 1.0

    def rep(v, n=P):
        return np.ascontiguousarray(np.broadcast_to(np.asarray(v, f32).reshape(1, -1), (n, np.asarray(v).size)))

    def col(v, nchunk):
        return np.ascontiguousarray(np.asarray(v, f32).reshape(nchunk, P).T)

    shared = {
        "ropec": ropec, "cst": cst,
        "gfrep": rep(inputs["norm_f_g"]),
        "lamrep": np.ascontiguousarray(np.stack([rep(inputs[k][l]) for k in
                                                 ("lambda_q1", "lambda_k1", "lambda_q2", "lambda_k2")], axis=1)),
        "slncol": col(inputs["subln_g"][l], 2),
        "g1col": col(inputs["norm1_g"][l], c.DC), "g2col": col(inputs["norm2_g"][l], c.DC),
        "pscol": col(inputs["pool_scale"][l], 4 * c.PGC),
        "cwcol": np.ascontiguousarray(np.asarray(inputs["conv_w"][l], f32).reshape(3, c.FC, P).transpose(2, 1, 0)),
        "cbcol": col(inputs["conv_b"][l], c.FC),
        "w_in": np.ascontiguousarray(inputs["w_in"][l], f32), "w_pool": np.ascontiguousarray(inputs["w_pool"][l], f32),
        "w_out": np.ascontiguousarray(inputs["w_out"][l], f32), "w_gate": np.ascontiguousarray(inputs["w_gate"][l], f32),
        "w_up": np.ascontiguousarray(inputs["w_up"][l], f32), "w_down": np.ascontiguousarray(inputs["w_down"][l], f32),
    }
    in_maps = []
    for b in range(c.BATCH):
        for half in range(2):
            seqpos = np.arange(c.SEQ) - (0 if half == 1 else c.HALF)
            valid = seqpos >= 0
            xs_ = np.zeros((c.SEQ, c.D), f32)
            pos = np.zeros((c.SEQ,), np.int32)
            if half == 1:
                xs_[:] = x[b]
                pos[:] = positions[b]
            else:
                xs_[c.HALF:] = x[b, :c.HALF]
                pos[c.HALF:] = positions[b, :c.HALF]
            kv = valid.astype(f32).reshape(c.NB, P).T
            own = seqpos[c.OWN_S0:]
            ic = np.ones((4, own.size), f32)
            for g, w in enumerate((2, 4, 8, 16)):
                cnt = np.minimum(np.maximum(own + 1, 1), w).astype(f32)
                ic[g] = f32(1.0) / cnt
            m = dict(shared)
            m["xs"] = xs_
            m["posr"] = np.ascontiguousarray(np.broadcast_to(pos[None, :], (32, c.SEQ)))
            m["kvalid"] = np.ascontiguousarray(kv)
            m["invcnt"] = np.ascontiguousarray(np.broadcast_to(ic[None], (P, 4, own.size)))
            in_maps.append(m)
    return in_maps


_NC_CACHE = {}


def run(cfg, inputs, trace=False):
    key = (cfg.D, cfg.SEQ, cfg.DFF, cfg.BATCH)
    if key not in _NC_CACHE:
        _NC_CACHE[key] = build(cfg)
    nc = _NC_CACHE[key]
    in_maps = host_inputs(cfg, inputs)
    ncores = len(in_maps)
    res = run_bass_kernel_spmd(nc, in_maps, core_ids=list(range(ncores)), trace=trace)
    out = np.zeros((cfg.BATCH, cfg.SEQ, cfg.D), np.float32)
    for b in range(cfg.BATCH):
        for half in range(2):
            out[b, half * cfg.HALF:(half + 1) * cfg.HALF] = res.results[b * 2 + half]["y"]
    return out, res


def kernel(**inputs):
    cfg = Cfg()
    out, _ = run(cfg, inputs)
    return out
```
